# Optimizing a Trainium2 kernel written in Bass

```python
import math
import jax, jax.numpy as jnp
from jax import lax
import numpy as np

D_MODEL = 2048
BATCH = 32
SEQ = 256
DEPTH = 2
DEC_BATCH = 2
DEC_SEQ = 2048
PAST_LEN = 256

GRID_W = 64
HEAD_DIM = 128
N_HEADS_A = 8
N_KV_A = 2
G_A = N_HEADS_A // N_KV_A
N_HEADS_B = 8
N_KV_B = 2
G_B = N_HEADS_B // N_KV_B
WINDOW = 128
BLOCK = 128
D_FF = 5632
CONV_K = 3
ROPE_THETA = 10000.0
LN_EPS = 1e-6
ALPHA = (2.0 * DEPTH) ** 0.25
BETA = (8.0 * DEPTH) ** -0.25
SCALE = HEAD_DIM ** -0.5
NEG = -1e30
SIZES = (N_HEADS_A * HEAD_DIM, N_KV_A * HEAD_DIM, N_KV_A * HEAD_DIM,
         N_HEADS_B * HEAD_DIM, N_KV_B * HEAD_DIM, N_KV_B * HEAD_DIM)
D_IN = sum(SIZES)
SPLITS = tuple(int(s) for s in np.cumsum(SIZES)[:-1])

kernel_name = "hybrid_dit_window_sink_axialrope_convffn_step"


def layer_norm(x, g=None, b=None):
    xf = x.astype(jnp.float32)
    mu = jnp.mean(xf, axis=-1, keepdims=True)
    var = jnp.mean(jnp.square(xf - mu), axis=-1, keepdims=True)
    y = (xf - mu) * lax.rsqrt(var + LN_EPS)
    if g is not None:
        y = y * g.astype(jnp.float32) + b.astype(jnp.float32)
    return y.astype(x.dtype)


def rms_norm(x, g):
    xf = x.astype(jnp.float32)
    y = xf * lax.rsqrt(jnp.mean(jnp.square(xf), axis=-1, keepdims=True) + LN_EPS)
    return (y * g.astype(jnp.float32)).astype(x.dtype)


def adaln(cvec, w, b):
    mod = (jax.nn.silu(cvec) @ w + b).reshape(cvec.shape[0], 1, 6, D_MODEL)
    return [mod[:, :, i] for i in range(6)]


def modulate(x, shift, scale):
    return layer_norm(x) * (1.0 + scale) + shift


def post_norm(x, out, gate, g, b):
    return layer_norm(ALPHA * x + gate * out, g, b)


def rope_tables(n):
    rows = n // GRID_W
    row = jnp.repeat(jnp.arange(rows), GRID_W).astype(jnp.float32)
    col = jnp.tile(jnp.arange(GRID_W), rows).astype(jnp.float32)
    q4 = HEAD_DIM // 4
    freq = ROPE_THETA ** (-jnp.arange(q4, dtype=jnp.float32) / q4)
    ang = jnp.stack([row[:, None] * freq, col[:, None] * freq], axis=1)
    return jnp.cos(ang), jnp.sin(ang)


def apply_rope(x, cos, sin):
    xs = x.astype(jnp.float32).reshape(x.shape[:-1] + (2, 2, HEAD_DIM // 4))
    x1, x2 = xs[..., 0, :], xs[..., 1, :]
    c, s = cos[None, :, None], sin[None, :, None]
    out = jnp.stack([x1 * c - x2 * s, x2 * c + x1 * s], axis=-2)
    return out.reshape(x.shape).astype(x.dtype)


def project(h, w_in, qn_g, kn_g):
    B, N = h.shape[0], h.shape[1]
    qa, ka, va, qb, kb, vb = jnp.split(h @ w_in, SPLITS, axis=-1)
    qa = qa.reshape(B, N, N_HEADS_A, HEAD_DIM)
    ka = ka.reshape(B, N, N_KV_A, HEAD_DIM)
    va = va.reshape(B, N, N_KV_A, HEAD_DIM)
    qb = rms_norm(qb.reshape(B, N, N_HEADS_B, HEAD_DIM), qn_g)
    kb = rms_norm(kb.reshape(B, N, N_KV_B, HEAD_DIM), kn_g)
    vb = vb.reshape(B, N, N_KV_B, HEAD_DIM)
    return qa, ka, va, qb, kb, vb


def dense_attend(q, k, v, sink=None):
    s = jnp.einsum('bqkgd,bskd->bkgqs', q, k).astype(jnp.float32) * SCALE
    if sink is not None:
        col = jnp.broadcast_to(sink.astype(jnp.float32)[None, :, :, None, None], s.shape[:-1] + (1,))
        s = jnp.concatenate([s, col], axis=-1)
    p = jax.nn.softmax(s, axis=-1)
    if sink is not None:
        p = p[..., :-1]
    return jnp.einsum('bkgqs,bskd->bqkgd', p.astype(v.dtype), v)


def _bands(t):
    B, N = t.shape[0], t.shape[1]
    nb = N // BLOCK
    tp = jnp.pad(t, ((0, 0), (BLOCK, BLOCK), (0, 0), (0, 0)))
    tb = tp.reshape(B, nb + 2, BLOCK, t.shape[2], HEAD_DIM)
    return jnp.concatenate([tb[:, :-2], tb[:, 1:-1], tb[:, 2:]], axis=2)


def window_attend_latent(q, k, v, ck, cv, sink):
    B, N = q.shape[0], q.shape[1]
    nb = N // BLOCK
    L = 3 * BLOCK
    P = ck.shape[1]
    qb = q.reshape(B, nb, BLOCK, N_KV_A, G_A, HEAD_DIM)
    kb, vb = _bands(k), _bands(v)
    s_loc = jnp.einsum('bnqkgd,bnskd->bnkgqs', qb, kb).astype(jnp.float32) * SCALE
    s_ctx = jnp.einsum('bnqkgd,bskd->bnkgqs', qb, ck).astype(jnp.float32) * SCALE
    a = jnp.arange(BLOCK)[:, None]
    j = jnp.arange(L)[None, :]
    near = jnp.abs(a - j + BLOCK) <= WINDOW
    kpos = jnp.arange(nb)[:, None, None] * BLOCK - BLOCK + j[None]
    mask = near[None] & (kpos >= 0) & (kpos < N)
    s_loc = jnp.where(mask[None, :, None, None], s_loc, NEG)
    s_sink = jnp.broadcast_to(sink.astype(jnp.float32)[None, None, :, :, None, None], s_loc.shape[:-1] + (1,))
    p = jax.nn.softmax(jnp.concatenate([s_loc, s_ctx, s_sink], axis=-1), axis=-1).astype(v.dtype)
    o = (jnp.einsum('bnkgqs,bnskd->bnqkgd', p[..., :L], vb)
         + jnp.einsum('bnkgqs,bskd->bnqkgd', p[..., L:L + P], cv))
    return o.reshape(B, N, N_HEADS_A * HEAD_DIM)


def global_attend_latent(q, k, v, ck, cv):
    B, N = q.shape[0], q.shape[1]
    nb = N // BLOCK
    keys = jnp.concatenate([ck, k], axis=1)
    vals = jnp.concatenate([cv, v], axis=1)
    qb = jnp.moveaxis(q.reshape(B, nb, BLOCK, N_KV_B, G_B, HEAD_DIM), 1, 0)
    out = lax.map(lambda qq: dense_attend(qq, keys, vals), qb)
    return jnp.moveaxis(out, 0, 1).reshape(B, N, N_HEADS_B * HEAD_DIM)


def mixers_context(h, w_in, qn_g, kn_g, sink):
    B, N = h.shape[0], h.shape[1]
    qa, ka, va, qb, kb, vb = project(h, w_in, qn_g, kn_g)
    oa = dense_attend(qa.reshape(B, N, N_KV_A, G_A, HEAD_DIM), ka, va, sink.reshape(N_KV_A, G_A))
    ob = dense_attend(qb.reshape(B, N, N_KV_B, G_B, HEAD_DIM), kb, vb)
    o = jnp.concatenate([oa.reshape(B, N, -1), ob.reshape(B, N, -1)], axis=-1)
    return o, jnp.stack([ka, va], axis=1), jnp.stack([kb, vb], axis=1)


def mixers_latent(h, cache_a, cache_b, w_in, qn_g, kn_g, sink):
    B, N = h.shape[0], h.shape[1]
    cos, sin = rope_tables(N)
    qa, ka, va, qb, kb, vb = project(h, w_in, qn_g, kn_g)
    qa, ka = apply_rope(qa, cos, sin), apply_rope(ka, cos, sin)
    qb, kb = apply_rope(qb, cos, sin), apply_rope(kb, cos, sin)
    oa = window_attend_latent(qa.reshape(B, N, N_KV_A, G_A, HEAD_DIM), ka, va,
                              cache_a[:, 0], cache_a[:, 1], sink.reshape(N_KV_A, G_A))
    ob = global_attend_latent(qb.reshape(B, N, N_KV_B, G_B, HEAD_DIM), kb, vb,
                              cache_b[:, 0], cache_b[:, 1])
    return jnp.concatenate([oa, ob], axis=-1)


def conv_ffn(h, w_up, conv_w, conv_b, w_down):
    u = h @ w_up
    up = jnp.pad(u, ((0, 0), (1, 1), (0, 0)))
    u = up[:, :-2] * conv_w[0] + up[:, 1:-1] * conv_w[1] + up[:, 2:] * conv_w[2] + conv_b
    val, gate = jnp.split(u, 2, axis=-1)
    return (jax.nn.silu(gate) * val) @ w_down


def setup_inputs(seed: int = 0) -> dict:
    key = jax.random.key(seed)
    ks = jax.random.split(key, 24)
    f32 = jnp.float32
    nrm = lambda k, shape, s: jax.random.normal(k, shape, f32) * s
    D, F = D_MODEL, D_FF
    return {
        'x_prompt': nrm(ks[0], (BATCH, SEQ, D), 1.0),
        'x_sample': nrm(ks[1], (DEC_BATCH, DEC_SEQ, D), 1.0),
        'cache_attn_a': nrm(ks[2], (DEC_BATCH, DEPTH, 2, PAST_LEN, N_KV_A, HEAD_DIM), 1.0),
        'cache_attn_b': nrm(ks[3], (DEC_BATCH, DEPTH, 2, PAST_LEN, N_KV_B, HEAD_DIM), 1.0),
        'c': nrm(ks[4], (DEC_BATCH, D), 1.0),
        'c_ctx': nrm(ks[5], (D,), 1.0),
        'w_ada': nrm(ks[6], (DEPTH, D, 6 * D), 0.5 * D ** -0.5),
        'b_ada': nrm(ks[7], (DEPTH, 6 * D), 0.02),
        'w_in': nrm(ks[8], (DEPTH, D, D_IN), D ** -0.5),
        'q_norm_g': 1.0 + nrm(ks[9], (DEPTH, HEAD_DIM), 0.02),
        'k_norm_g': 1.0 + nrm(ks[10], (DEPTH, HEAD_DIM), 0.02),
        'sink_a': nrm(ks[11], (DEPTH, N_HEADS_A), 0.5),
        'w_o': nrm(ks[12], (DEPTH, D, D), BETA * D ** -0.5),
        'ln1_g': 1.0 + nrm(ks[13], (DEPTH, D), 0.02),
        'ln1_b': nrm(ks[14], (DEPTH, D), 0.02),
        'w_up': nrm(ks[15], (DEPTH, D, 2 * F), D ** -0.5),
        'conv_w': nrm(ks[16], (DEPTH, CONV_K, 2 * F), CONV_K ** -0.5),
        'conv_b': nrm(ks[17], (DEPTH, 2 * F), 0.02),
        'w_down': nrm(ks[18], (DEPTH, F, D), BETA * F ** -0.5),
        'ln2_g': 1.0 + nrm(ks[19], (DEPTH, D), 0.02),
        'ln2_b': nrm(ks[20], (DEPTH, D), 0.02),
    }


def reference(x_prompt, x_sample, cache_attn_a, cache_attn_b, c, c_ctx, w_ada, b_ada, w_in,
              q_norm_g, k_norm_g, sink_a, w_o, ln1_g, ln1_b, w_up, conv_w, conv_b, w_down,
              ln2_g, ln2_b):
    y = x_prompt
    z = x_sample
    new_a, new_b = [], []
    for l in range(DEPTH):
        s1, sc1, g1, s2, sc2, g2 = adaln(c_ctx[None], w_ada[l], b_ada[l])
        h = modulate(y, s1, sc1)
        o, kv_a, kv_b = mixers_context(h, w_in[l], q_norm_g[l], k_norm_g[l], sink_a[l])
        y = post_norm(y, o @ w_o[l], g1, ln1_g[l], ln1_b[l])
        h = modulate(y, s2, sc2)
        y = post_norm(y, conv_ffn(h, w_up[l], conv_w[l], conv_b[l], w_down[l]), g2, ln2_g[l], ln2_b[l])
        new_a.append(kv_a)
        new_b.append(kv_b)
        s1, sc1, g1, s2, sc2, g2 = adaln(c, w_ada[l], b_ada[l])
        h = modulate(z, s1, sc1)
        o = mixers_latent(h, cache_attn_a[:, l], cache_attn_b[:, l], w_in[l], q_norm_g[l], k_norm_g[l], sink_a[l])
        z = post_norm(z, o @ w_o[l], g1, ln1_g[l], ln1_b[l])
        h = modulate(z, s2, sc2)
        z = post_norm(z, conv_ffn(h, w_up[l], conv_w[l], conv_b[l], w_down[l]), g2, ln2_g[l], ln2_b[l])
    new_cache_a = jnp.stack(new_a, axis=1)
    new_cache_b = jnp.stack(new_b, axis=1)
    return (y, z, new_cache_a, new_cache_b)
```

```python
import math
from contextlib import ExitStack

import numpy as np
import concourse.bass as bass
import concourse.mybir as mybir
from concourse.bass_utils import run_bass_kernel_spmd

F32 = mybir.dt.float32
BF16 = mybir.dt.bfloat16
ALU = mybir.AluOpType
AF = mybir.ActivationFunctionType
AX = mybir.AxisListType

D = 2048
L = 2
NCORES = 8
SEQ = 256
DFF = 5632
NFT = DFF // 128
HD = 128
EPS = 1e-6
ALPHA = (2.0 * L) ** 0.25
SCALE = HD ** -0.5
GRID_W = 64
ROPE_THETA = 10000.0
QKV_ORDER = (2, 5, 0, 1, 3, 4)
QKV_COL = (0, 512, 1024, 1536, 2048, 2560)
RG = [[0, 1, 2, 3], [4, 5, 6, 7]]


class Buf:
    __slots__ = ("name", "last_w", "readers", "excl")

    def __init__(self, name, excl=False):
        self.name = name
        self.last_w = None
        self.readers = {}
        self.excl = excl


class EngQ:
    def __init__(self, name, sem):
        self.name = name
        self.sem = sem
        self.count = 0
        self.ops = []
        self.waited = {}


class DSem:
    def __init__(self, sem):
        self.sem = sem
        self.count = 0


class StopBuild(Exception):
    pass


class Sched:
    def __init__(self, nc, stack):
        self.nc = nc
        self.stack = stack
        self.q = {}
        for n in ("pe", "act", "dve", "pool", "sp"):
            self.q[n] = EngQ(n, stack.enter_context(nc.semaphore("sem_" + n)))
        self.n_dsem = 0
        self.all_dsems = []

    def dsem(self, name=None):
        self.n_dsem += 1
        d = DSem(self.stack.enter_context(self.nc.semaphore(name or f"dsem{self.n_dsem}")))
        self.all_dsems.append(d)
        return d

    def _wait(self, q, tok):
        if tok is None:
            return
        sem, val, owner = tok
        if owner == "pe" and q.name == "pe":
            return
        key = id(sem)
        if q.waited.get(key, 0) >= val:
            return
        q.waited[key] = val
        q.ops.append(lambda e, sem=sem, val=val: e.wait_ge(sem, val))

    def _deps(self, q, reads, writes):
        for b in reads:
            self._wait(q, b.last_w)
            if b.excl:
                for r in list(b.readers.values()):
                    if r[2] != q.name:
                        self._wait(q, r)
        for b in writes:
            self._wait(q, b.last_w)
            for r in list(b.readers.values()):
                self._wait(q, r)

    @staticmethod
    def _addr(d, tok):
        k = id(tok[0])
        if k not in d or d[k][1] < tok[1]:
            d[k] = tok

    def _commit(self, tok, reads, writes):
        for b in reads:
            self._addr(b.readers, tok)
        for b in writes:
            b.last_w = tok
            b.readers = {}

    def op(self, eng, fn, reads=(), writes=()):
        q = self.q[eng]
        self._deps(q, reads, writes)
        q.count += 1
        val = q.count
        sem = q.sem
        q.ops.append(lambda e, fn=fn, sem=sem: fn(e).then_inc(sem, 1))
        tok = (sem, val, eng)
        self._commit(tok, reads, writes)
        return tok

    def group(self, eng, fns, reads=(), writes=()):
        q = self.q[eng]
        self._deps(q, reads, writes)
        for fn in fns[:-1]:
            q.ops.append(lambda e, fn=fn: fn(e))
        q.count += 1
        val = q.count
        sem = q.sem
        q.ops.append(lambda e, fn=fns[-1], sem=sem: fn(e).then_inc(sem, 1))
        tok = (sem, val, eng)
        self._commit(tok, reads, writes)
        return tok

    def dma(self, eng, ds, fns, reads=(), writes=()):
        q = self.q[eng]
        self._deps(q, reads, writes)
        for fn in fns:
            ds.count += 16
            q.ops.append(lambda e, fn=fn, sem=ds.sem: fn(e).then_inc(sem, 16))
        tok = (ds.sem, ds.count, "dma")
        self._commit(tok, reads, writes)
        return tok

    def custom(self, eng, fn, sem, val, reads=(), writes=()):
        q = self.q[eng]
        self._deps(q, reads, writes)
        q.ops.append(lambda e, fn=fn: fn(e))
        tok = (sem, val, "custom")
        self._commit(tok, reads, writes)
        return tok

    def fence(self, dst_bufs, src_bufs):
        toks = []
        for s in src_bufs:
            if s.last_w is not None:
                toks.append(s.last_w)
            toks.extend(s.readers.values())
        for d in dst_bufs:
            for t in toks:
                self._addr(d.readers, t)

    def final_wait(self, eng, toks):
        q = self.q[eng]
        for t in toks:
            self._wait(q, t)

    def emit(self, block):
        qs = self.q

        @block.tensor
        def _(e):
            for f in qs["pe"].ops:
                f(e)

        @block.scalar
        def _(e):
            for f in qs["act"].ops:
                f(e)

        @block.vector
        def _(e):
            for f in qs["dve"].ops:
                f(e)

        @block.gpsimd
        def _(e):
            for f in qs["pool"].ops:
                f(e)

        @block.sync
        def _(e):
            for f in qs["sp"].ops:
                f(e)


def ada_l1_order():
    return [(i, cb) for i in range(6) for cb in range(4)]


def weight_sequence(depth, groups):
    seq = []
    for i in range(6):
        for cb in range(4):
            seq.append(("ada", 0, i, cb))
    for gn, gi in enumerate(groups):
        for l in range(depth):
            for cbi in QKV_ORDER:
                seq.append(("in", l, cbi))
            for cb in range(4):
                seq.append(("o", l, cb))
            for c in range(11):
                seq.append(("up", l, c))
                seq.append(("up", l, 11 + c))
            n = 0
            for cb in range(4):
                for sub in range(3):
                    seq.append(("down", l, cb, sub))
                    if gn == 0 and l == 0 and depth > 1:
                        for (i2, cb2) in ada_l1_order()[n:n + 2]:
                            seq.append(("ada", 1, i2, cb2))
                        n += 2
    return seq


def build_program(depth=L, groups=(0, 1, 2), stop=None, dbg=()):
    nc = bass.Bass("TRN2", target_bir_lowering=False)

    def din(n, shp, dt=F32):
        return nc.dram_tensor(n, shp, dt, kind="ExternalInput").ap()

    def dout(n, shp, dt=F32):
        return nc.dram_tensor(n, shp, dt, kind="ExternalOutput").ap()

    xc = din("xc", [1024, D])
    xl = din("xl", [512, D])
    cch = din("cch", [L, 2, 2, 256, 256])
    cT_d = din("cT", [128, 16 * 33])
    w_ada = din("w_ada", [L, 24, 128, 8192])
    b_ada = din("b_ada", [L, 6 * D])
    w_in = din("w_in", [L, 6, 128, 8192])
    w_o = din("w_o", [L, 4, 128, 8192])
    w_up = din("w_up", [L, 22, 128, 8192])
    w_down = din("w_down", [L, 4, 128, NFT * 512])
    qg_d = din("qg", [L, 128])
    kg_d = din("kg", [L, 128])
    sink_d = din("sink", [L, 8])
    ln1g_d = din("ln1g", [L, D])
    ln1b_d = din("ln1b", [L, D])
    ln2g_d = din("ln2g", [L, D])
    ln2b_d = din("ln2b", [L, D])
    cw_d = din("cw", [L, 128, 4 * 88])
    ropec_d = din("ropec", [128, 256])
    ropes_d = din("ropes", [128, 256])
    masks_d = din("masks", [128, 512])
    sel_d = din("sel", [128, 8])
    ident_d = din("ident", [128, 128])

    yc = dout("yc", [1024, D])
    yl = dout("yl", [512, D])
    nca = dout("nca", [4, L, 2, 256, 256])
    ncb = dout("ncb", [4, L, 2, 256, 256])

    modd_t = nc.dram_tensor("modd", [L, 2, 6 * D], F32)
    kvst_t = nc.dram_tensor("kvst", [128, 4096], BF16)
    kvg_t = nc.dram_tensor("kvg", [512, 4096], BF16)
    est_t = nc.dram_tensor("est", [128, 176], F32)
    eg_t = nc.dram_tensor("eg", [512, 176], F32)
    modd = modd_t.ap()
    kvst = kvst_t.ap()
    kvg = kvg_t.ap()
    est = est_t.ap()
    eg = eg_t.ap()

    dbg_outs = {}

    with ExitStack() as st:
        S = Sched(nc, st)

        def sb(n, shp, dt):
            return st.enter_context(nc.sbuf_tensor("s_" + n, shp, dt))

        def ps(n, shp, dt):
            return st.enter_context(nc.psum_tensor("p_" + n, shp, dt))

        X = sb("X", [128, 4, D], F32)
        XH = sb("XH", [128, D], F32)
        hT = sb("hT", [128, 16, 512], BF16)
        R = sb("R", [128, 24576], BF16)
        kT = sb("kT", [128, 4, 512], BF16)
        vv = sb("vv", [128, 4, 4, 128], BF16)
        wr = [sb(f"wr{i}", [128, 16, 512], BF16) for i in range(2)]
        GT = sb("GT", [128, D], F32)
        LG = sb("LG", [128, D], F32)
        LB = sb("LB", [128, D], F32)
        idf = sb("idf", [128, 128], F32)
        idb = sb("idb", [128, 128], BF16)
        ones_bf = sb("ones_bf", [128, 128], BF16)
        masks = sb("masks", [128, 512], BF16)
        ropec = sb("ropec", [128, 256], F32)
        ropes = sb("ropes", [128, 256], F32)
        sel = sb("sel", [128, 8], F32)
        mod_in = sb("mod_in", [96, 128], F32)
        modT = sb("modT", [128, 96], F32)
        cw = sb("cw", [128, 4 * 88], F32)
        gq_b = sb("gq_b", [128, 128], F32)
        gk_b = sb("gk_b", [128, 128], F32)
        esink = sb("esink", [128, 8], F32)
        cv_sb = sb("cv_sb", [128, 2, 2, 256], BF16)
        ck_sb = sb("ck_sb", [128, 2, 2, 256], BF16)
        ckT = sb("ckT", [128, 4, 256], BF16)
        cT = sb("cT", [128, 16 * 33], F32)
        sT = sb("sT", [128, 16, 33], BF16)
        tmpb = [sb(f"tmpb{i}", [128, 512], BF16) for i in range(2)]
        ft = [sb(f"ft{i}", [128, 512], F32) for i in range(4)]
        pT = [sb(f"pT{i}", [128, 512], BF16) for i in range(3)]
        HS = sb("HS", [128, 1024], BF16)
        E_own = sb("E_own", [128, 88, 2], F32)
        P_edge = sb("P_edge", [128, 88, 2], F32)
        G2 = sb("G2", [128, 4, 176], F32)
        hlr = sb("hlr", [128, 2, 88], F32)
        etmp = sb("etmp", [128, 88], F32)
        stt = [sb(f"stt{i}", [128, 4, 6], F32) for i in range(4)]
        mv = [sb(f"mv{i}", [128, 8], F32) for i in range(4)]
        ss = sb("ss", [128, 16], F32)

        mm = [ps(f"mm{i}", [128, 512], F32) for i in range(4)]
        tr = [ps(f"tr{i}", [128, 1024], BF16) for i in range(2)]
        att = [ps(f"att{i}", [128, 512], F32) for i in range(2)]

        qT_v = R[:, 0:8192].rearrange("p (h t) -> p h t", h=16)
        GB_v = R[:, 8192:16384].rearrange("p (r x) -> p r x", r=4)
        H_v = R[:, 16384:20480].rearrange("p (r x) -> p r x", r=4)
        aT_v = R[:, 0:22528].rearrange("p (j t) -> p j t", j=NFT)

        Xb = [Buf(f"X{t}") for t in range(4)]
        XHb = Buf("XH")
        hTb = [Buf(f"hT{t}") for t in range(4)]
        qTb = [Buf(f"qT{t}") for t in range(4)]
        kTb = [Buf(f"kT{t}") for t in range(4)]
        vb = [Buf(f"v{t}") for t in range(4)]
        GBb = Buf("GB")
        Hb = Buf("H")
        aTb = [Buf(f"aT{j}") for j in range(NFT)]
        wrb = [Buf("wr0"), Buf("wr1")]
        GTb, LGb, LBb = Buf("GT"), Buf("LG"), Buf("LB")
        constb = Buf("const")
        mod_inb, modTb, cwb, gqb, gkb, esinkb = Buf("mod_in"), Buf("modT"), Buf("cw"), Buf("gq"), Buf("gk"), Buf("esink")
        auxb = Buf("aux")
        cvb = ckb = auxb
        ckTb = Buf("ckT")
        cTb, sTb = Buf("cT"), Buf("sT")
        tmpbb = [Buf("tmpb0"), Buf("tmpb1")]
        ftb = [Buf(f"ft{i}") for i in range(4)]
        pTb = [Buf(f"pT{i}") for i in range(3)]
        HSb, Eb, Pb, G2b, hlrb, etmpb = Buf("HS"), Buf("E"), Buf("P"), Buf("G2"), Buf("hlr"), Buf("etmp")
        sttb = [Buf(f"stt{i}") for i in range(4)]
        mvb = [Buf(f"mv{i}") for i in range(4)]
        ssb = Buf("ss")
        mmb = [Buf(f"mm{i}", excl=True) for i in range(4)]
        trb = [Buf(f"tr{i}", excl=True) for i in range(2)]
        attb = [Buf(f"att{i}", excl=True) for i in range(2)]
        moddb2 = [Buf("modd0"), Buf("modd1")]
        kvstb, kvgb, estb, egb = Buf("kvst"), Buf("kvg"), Buf("est"), Buf("eg")
        outb = Buf("out")

        ctr = {"mm": 0, "tr": 0, "ft": 0, "tmpb": 0, "pT": 0, "stat": 0, "cc": 0, "nrow": 0}

        def nxt(key, n):
            i = ctr[key] % n
            ctr[key] += 1
            return i

        def next_mm():
            i = nxt("mm", 4)
            return mm[i], mmb[i]

        def next_tr():
            i = nxt("tr", 2)
            return tr[i], trb[i]

        ft_idx = {}

        def next_ft():
            i = nxt("ft", 4)
            ft_idx[id(ft[i])] = i
            return ft[i], ftb[i]

        def next_tmpb():
            i = nxt("tmpb", 2)
            return tmpb[i], tmpbb[i]

        def next_pT():
            i = nxt("pT", 3)
            return pT[i], pTb[i]

        out_toks = []
        ccsem = st.enter_context(nc.semaphore("ccsem"))

        ds_x = [S.dsem(f"ds_x{t}") for t in range(4)]
        ds_wr = [[S.dsem(f"ds_wr{i}_{j}") for j in range(2)] for i in range(2)]
        ds_c = S.dsem("ds_const")
        ds_gt, ds_lg, ds_lb = S.dsem("ds_gt"), S.dsem("ds_lg"), S.dsem("ds_lb")
        ds_mod, ds_cw, ds_gq, ds_gk, ds_sink = S.dsem("ds_mod"), S.dsem("ds_cw"), S.dsem("ds_gq"), S.dsem("ds_gk"), S.dsem("ds_sink")
        ds_aux = S.dsem("ds_aux")
        ds_cv = ds_ck = ds_aux
        ds_out = S.dsem("ds_out")
        ds_modd = [S.dsem("ds_modd0"), S.dsem("ds_modd1")]
        ds_fo = [S.dsem(f"ds_fo{i}") for i in range(4)]
        ds_xo = [S.dsem(f"ds_xo{t}") for t in range(4)]
        ds_bias = [S.dsem("ds_bias0"), S.dsem("ds_bias1")]
        ds_kvst, ds_gb, ds_h = S.dsem("ds_kvst"), S.dsem("ds_gb"), S.dsem("ds_h")
        ds_est, ds_g2 = S.dsem("ds_est"), S.dsem("ds_g2")
        ds_dbg = S.dsem("ds_dbg")

        wseq = weight_sequence(depth, groups)
        wst = {"load": 0, "use": 0}

        def w_src(desc):
            kind = desc[0]
            if kind == "ada":
                _, l, i, cb = desc
                return w_ada[l, i * 4 + cb, :, :], 8192
            if kind == "in":
                _, l, cbi = desc
                return w_in[l, cbi, :, :], 8192
            if kind == "o":
                _, l, cb = desc
                return w_o[l, cb, :, :], 8192
            if kind == "up":
                _, l, c = desc
                return w_up[l, c, :, :], 8192
            _, l, cb, sub = desc
            k0 = sub * 16
            nk = min(16, NFT - k0)
            return w_down[l, cb, :, k0 * 512:(k0 + nk) * 512], nk * 512

        def w_issue():
            i = wst["load"]
            slot = i % 2
            src, n = w_src(wseq[i])
            dst = wr[slot][:].rearrange("p k c -> p (k c)")
            fns = []
            for c0 in range(0, n, 4096):
                c1 = min(n, c0 + 4096)
                fns.append(lambda e, c0=c0, c1=c1, src=src, dst=dst: e.dma_start(out=dst[:, c0:c1], in_=src[:, c0:c1]))
            S.dma("pool", ds_wr[slot][(i // 2) % 2], fns, writes=[wrb[slot]])
            wst["load"] += 1

        def w_next(desc):
            i = wst["use"]
            assert wseq[i] == desc, (wseq[i], desc)
            while wst["load"] < min(i + 2, len(wseq)):
                w_issue()
            wst["use"] += 1
            return wr[i % 2], wrb[i % 2]

        def check_stop(tag):
            if stop is not None and tuple(stop) == tuple(tag):
                raise StopBuild()

        try:
            S.dma("sp", ds_c, [
                lambda e: e.dma_start(out=idf[:], in_=ident_d[:, :]),
                lambda e: e.dma_start(out=ropec[:], in_=ropec_d[:, :]),
                lambda e: e.dma_start(out=ropes[:], in_=ropes_d[:, :]),
                lambda e: e.dma_start(out=sel[:], in_=sel_d[:, :]),
                lambda e: e.dma_start(out=cT[:], in_=cT_d[:, :]),
            ], writes=[constb, cTb])
            maskb = auxb
            S.dma("pool", ds_aux, [lambda e: e.dma_start(out=masks[:], in_=masks_d[:, :])], writes=[maskb])
            S.op("dve", lambda e: e.tensor_copy(out=idb[:], in_=idf[:]), reads=[constb, maskb], writes=[constb])
            S.op("dve", lambda e: e.memset(ones_bf[:], 1.0), writes=[constb])
            S.op("act", lambda e: e.activation(out=sT[:].rearrange("p k m -> p (k m)"), in_=cT[:], func=AF.Silu),
                 reads=[cTb], writes=[sTb])

            rowt = [XH[0:33, 0:512], XH[0:33, 512:1024]]
            biast = [XH[0:33, 1024:1536], XH[0:33, 1536:2048]]
            rowb = [Buf("rowt0"), Buf("rowt1")]
            biasb = [Buf("biast0"), Buf("biast1")]
            S.op("dve", lambda e: e.memset(XH[0:33, :], 0.0), writes=biasb + rowb)
            ctr["nrow"] = 0

            def ada_block(l, i, cb, bank, bb):
                wt, wb_ = w_next(("ada", l, i, cb))
                r = ctr["nrow"] % 2
                ctr["nrow"] += 1
                c0 = i * D + cb * 512
                S.dma("sp", ds_bias[r], [
                    lambda e: e.dma_start(out=biast[r][0:1, :], in_=b_ada[l:l + 1, c0:c0 + 512]),
                    lambda e: e.dma_start(out=biast[r][32:33, :], in_=b_ada[l:l + 1, c0:c0 + 512]),
                ], writes=[biasb[r]])
                S.group("pe", [
                    (lambda e, k=k: e.matmul(bank[0:33, :], lhsT=sT[:, k, :], rhs=wt[:, k, :], start=(k == 0), stop=(k == 15)))
                    for k in range(16)], reads=[sTb, wb_], writes=[bb])
                S.op("dve", lambda e: e.tensor_tensor(out=rowt[r], in0=bank[0:33, :], in1=biast[r], op=ALU.add),
                     reads=[bb, biasb[r]], writes=[rowb[r]])
                S.dma("sp", ds_modd[r], [
                    lambda e: e.dma_start(out=modd[l, 0:1, c0:c0 + 512], in_=rowt[r][0:1, :]),
                    lambda e: e.dma_start(out=modd[l, 1:2, c0:c0 + 512], in_=rowt[r][32:33, :]),
                ], reads=[rowb[r]], writes=[moddb2[r]])

            for i in range(6):
                for cb in range(4):
                    bank, bb = next_mm()
                    ada_block(0, i, cb, bank, bb)
            S.fence([XHb], rowb + biasb)
            check_stop(("ada",))

            def ln_stats(x_ap, xbuf):
                i = nxt("stat", 4)
                s_t, s_b, m_t, m_b = stt[i], sttb[i], mv[i], mvb[i]
                for c4 in range(4):
                    S.op("dve", lambda e, c4=c4: e.bn_stats(out=s_t[:, c4, :], in_=x_ap[:, c4 * 512:(c4 + 1) * 512]),
                         reads=[xbuf], writes=[s_b])
                S.op("dve", lambda e: e.bn_aggr(out=m_t[:, 0:2], in_=s_t[:].rearrange("p a b -> p (a b)")),
                     reads=[s_b], writes=[m_b])
                S.op("dve", lambda e: e.tensor_scalar(out=m_t[:, 2:3], in0=m_t[:, 1:2], scalar1=EPS, scalar2=None, op0=ALU.add),
                     reads=[m_b], writes=[m_b])
                S.op("act", lambda e: e.activation(out=m_t[:, 3:4], in_=m_t[:, 2:3], func=AF.Sqrt), reads=[m_b], writes=[m_b])
                S.op("dve", lambda e: e.reciprocal(out=m_t[:, 4:5], in_=m_t[:, 3:4]), reads=[m_b], writes=[m_b])
                S.op("dve", lambda e: e.tensor_scalar(out=m_t[:, 5:6], in0=m_t[:, 0:1], scalar1=m_t[:, 4:5], scalar2=-1.0,
                                                      op0=ALU.mult, op1=ALU.mult), reads=[m_b], writes=[m_b])
                return m_t[:, 4:5], m_t[:, 5:6], m_b

            def ln_to_hT_all(sh_c, sc_c):
                sts = [ln_stats(X[:, t, :], Xb[t]) for t in range(4)]
                for t in range(4):
                    ln_to_hT(t, sh_c, sc_c, sts[t])

            def ln_to_hT(t, sh_c, sc_c, st_=None):
                x_ap = X[:, t, :]
                rstd, nmr, m_b = st_ if st_ is not None else ln_stats(x_ap, Xb[t])
                S.op("act", lambda e: e.activation(out=XH[:], in_=x_ap, func=AF.Identity, scale=rstd, bias=nmr),
                     reads=[Xb[t], m_b], writes=[XHb])
                for k4 in range(4):
                    bank, bb = next_mm()
                    S.group("pe", [
                        (lambda e, kk=kk, bank=bank, k4=k4: e.transpose(out=bank[:, kk * 128:(kk + 1) * 128],
                                                                       in_=XH[:, (k4 * 4 + kk) * 128:(k4 * 4 + kk + 1) * 128],
                                                                       identity=idf[:]))
                        for kk in range(4)], reads=[XHb, constb], writes=[bb])
                    for kk in range(4):
                        k = k4 * 4 + kk
                        if k4 % 2 == 0:
                            S.op("act", lambda e, kk=kk, k=k, bank=bank: e.activation(
                                out=hT[:, k, t * 128:(t + 1) * 128], in_=bank[:, kk * 128:(kk + 1) * 128], func=AF.Identity,
                                scale=modT[:, sc_c + k:sc_c + k + 1], bias=modT[:, sh_c + k:sh_c + k + 1]),
                                reads=[bb, modTb], writes=[hTb[t]])
                        else:
                            S.op("dve", lambda e, kk=kk, k=k, bank=bank: e.tensor_scalar(
                                out=hT[:, k, t * 128:(t + 1) * 128], in0=bank[:, kk * 128:(kk + 1) * 128],
                                scalar1=modT[:, sc_c + k:sc_c + k + 1], scalar2=modT[:, sh_c + k:sh_c + k + 1],
                                op0=ALU.mult, op1=ALU.add), reads=[bb, modTb], writes=[hTb[t]])

            def transpose_block(tmp_t, tmp_b, nblk, dst_ap, dst_bufs, eng="dve"):
                trt, trb_ = next_tr()
                S.group("pe", [
                    (lambda e, i=i: e.transpose(out=trt[:, i * 128:(i + 1) * 128], in_=tmp_t[:, i * 128:(i + 1) * 128],
                                                identity=idb[:]))
                    for i in range(nblk)], reads=[tmp_b, constb], writes=[trb_])
                src = trt[:, 0:nblk * 128].rearrange("p (h t) -> p h t", h=nblk)
                if eng == "dve":
                    S.op("dve", lambda e: e.tensor_copy(out=dst_ap, in_=src), reads=[trb_], writes=dst_bufs)
                else:
                    S.op("act", lambda e: e.activation(out=dst_ap, in_=src, func=AF.Copy), reads=[trb_], writes=dst_bufs)

            def rms_norm(bank, bb, nh, g_t, g_b, out_ap, out_bufs):
                sq, sqb = next_ft()
                S.op("act", lambda e: e.activation(out=sq[:, 0:nh * 128], in_=bank[:, 0:nh * 128], func=AF.Square),
                     reads=[bb], writes=[sqb])
                S.op("dve", lambda e: e.tensor_reduce(out=ss[:, 0:nh], in_=sq[:, 0:nh * 128].rearrange("p (h d) -> p h d", h=nh),
                                                      axis=AX.X, op=ALU.add), reads=[sqb], writes=[ssb])
                S.op("dve", lambda e: e.tensor_scalar(out=ss[:, 4:4 + nh], in0=ss[:, 0:nh], scalar1=1.0 / HD, scalar2=EPS,
                                                      op0=ALU.mult, op1=ALU.add), reads=[ssb], writes=[ssb])
                S.op("act", lambda e: e.activation(out=ss[:, 8:8 + nh], in_=ss[:, 4:4 + nh], func=AF.Sqrt), reads=[ssb], writes=[ssb])
                S.op("dve", lambda e: e.reciprocal(out=ss[:, 12:12 + nh], in_=ss[:, 8:8 + nh]), reads=[ssb], writes=[ssb])
                for h in range(nh):
                    S.op("dve", lambda e, h=h: e.scalar_tensor_tensor(
                        out=out_ap[:, h * 128:(h + 1) * 128], in0=bank[:, h * 128:(h + 1) * 128], scalar=ss[:, 12 + h:13 + h],
                        in1=g_t[:], op0=ALU.mult, op1=ALU.mult), reads=[bb, ssb, g_b], writes=out_bufs)

            def rope(src_ap, src_bufs, nh, t, dst_ap, dst_bufs):
                xv = src_ap.rearrange("p (h a b f) -> p h a b f", h=nh, a=2, b=2)
                ov = dst_ap.rearrange("p (h a b f) -> p h a b f", h=nh, a=2, b=2)
                x1, x2 = xv[:, :, :, 0, :], xv[:, :, :, 1, :]
                C = ropec[:, t * 64:(t + 1) * 64].rearrange("p (a f) -> p a f", a=2).unsqueeze(1).broadcast_to([128, nh, 2, 32])
                Sn = ropes[:, t * 64:(t + 1) * 64].rearrange("p (a f) -> p a f", a=2).unsqueeze(1).broadcast_to([128, nh, 2, 32])
                fa, fab = next_ft()
                fb_, fbb = next_ft()
                n = nh * 64
                t1 = fa[:, 0:n].rearrange("p (h a f) -> p h a f", h=nh, a=2)
                t2 = fa[:, 256:256 + n].rearrange("p (h a f) -> p h a f", h=nh, a=2)
                t3 = fb_[:, 0:n].rearrange("p (h a f) -> p h a f", h=nh, a=2)
                t4 = fb_[:, 256:256 + n].rearrange("p (h a f) -> p h a f", h=nh, a=2)
                S.op("dve", lambda e: e.tensor_tensor(out=t1, in0=x1, in1=C, op=ALU.mult), reads=src_bufs + [constb], writes=[fab])
                S.op("dve", lambda e: e.tensor_tensor(out=t2, in0=x2, in1=Sn, op=ALU.mult), reads=src_bufs + [constb], writes=[fab])
                S.op("dve", lambda e: e.tensor_tensor(out=t3, in0=x2, in1=C, op=ALU.mult), reads=src_bufs + [constb], writes=[fbb])
                S.op("dve", lambda e: e.tensor_tensor(out=t4, in0=x1, in1=Sn, op=ALU.mult), reads=src_bufs + [constb], writes=[fbb])
                S.op("dve", lambda e: e.tensor_tensor(out=ov[:, :, :, 0, :], in0=t1, in1=t2, op=ALU.subtract), reads=[fab], writes=dst_bufs)
                S.op("dve", lambda e: e.tensor_tensor(out=ov[:, :, :, 1, :], in0=t3, in1=t4, op=ALU.add), reads=[fbb], writes=dst_bufs)

            def load_layer_consts(l, sidx, lat):
                S.dma("sp", ds_mod, [lambda e: e.dma_start(out=mod_in[:], in_=modd[l, sidx, :].rearrange("(i p) -> i p", p=128))],
                      reads=moddb2, writes=[mod_inb])
                bank, bb = next_mm()
                S.group("pe", [lambda e: e.transpose(out=bank[:, 0:96], in_=mod_in[:], identity=idf[0:96, 0:96])],
                        reads=[mod_inb, constb], writes=[bb])
                S.op("dve", lambda e: e.tensor_copy(out=modT[:], in_=bank[:, 0:96]), reads=[bb], writes=[modTb])
                S.op("dve", lambda e: e.tensor_scalar(out=modT[:, 16:32], in0=modT[:, 16:32], scalar1=1.0, scalar2=None, op0=ALU.add),
                     reads=[modTb], writes=[modTb])
                S.op("dve", lambda e: e.tensor_scalar(out=modT[:, 64:80], in0=modT[:, 64:80], scalar1=1.0, scalar2=None, op0=ALU.add),
                     reads=[modTb], writes=[modTb])
                S.dma("sp", ds_cw, [lambda e: e.dma_start(out=cw[:], in_=cw_d[l, :, :])], writes=[cwb])
                S.dma("sp", ds_gq, [lambda e: e.dma_start(out=gq_b[:], in_=qg_d[l, :].partition_broadcast(128))], writes=[gqb])
                S.dma("sp", ds_gk, [lambda e: e.dma_start(out=gk_b[:], in_=kg_d[l, :].partition_broadcast(128))], writes=[gkb])
                S.dma("sp", ds_sink, [lambda e: e.dma_start(out=esink[:], in_=sink_d[l, :].partition_broadcast(128))], writes=[esinkb])
                S.op("act", lambda e: e.activation(out=esink[:], in_=esink[:], func=AF.Exp), reads=[esinkb], writes=[esinkb])
                if lat:
                    for m in range(2):
                        for tt in range(2):
                            S.dma("pool", ds_cv, [lambda e, m=m, tt=tt: e.dma_start(
                                out=cv_sb[:, tt, m, :], in_=cch[l, m, 1, tt * 128:(tt + 1) * 128, :])], writes=[cvb])
                            S.dma("pool", ds_ck, [lambda e, m=m, tt=tt: e.dma_start(
                                out=ck_sb[:, tt, m, :], in_=cch[l, m, 0, tt * 128:(tt + 1) * 128, :])], writes=[ckb])
                    for tt in range(2):
                        transpose_block(ck_sb[:, tt, :, :].rearrange("p m c -> p (m c)"), ckb, 4,
                                        ckT[:, :, tt * 128:(tt + 1) * 128], [ckTb], eng="act")

            def load_bcast(l, sidx, which):
                gi_ = 2 if which == 1 else 5
                g_d, b_d = (ln1g_d, ln1b_d) if which == 1 else (ln2g_d, ln2b_d)
                S.dma("sp", ds_gt, [lambda e: e.dma_start(out=GT[:], in_=modd[l, sidx, gi_ * D:(gi_ + 1) * D].partition_broadcast(128))],
                      reads=moddb2, writes=[GTb])
                S.dma("sp", ds_lg, [lambda e: e.dma_start(out=LG[:], in_=g_d[l, :].partition_broadcast(128))], writes=[LGb])
                S.dma("sp", ds_lb, [lambda e: e.dma_start(out=LB[:], in_=b_d[l, :].partition_broadcast(128))], writes=[LBb])

            def resid_update(bank, bb, t, cb, first=True):
                tmp, tmpb_ = next_ft()
                xs = X[:, t, cb * 512:(cb + 1) * 512]
                S.op("dve", lambda e: e.tensor_tensor(out=tmp[:], in0=bank[:], in1=GT[:, cb * 512:(cb + 1) * 512], op=ALU.mult),
                     reads=[bb, GTb], writes=[tmpb_])
                S.op("dve", lambda e: e.scalar_tensor_tensor(out=xs, in0=xs, scalar=(ALPHA if first else 1.0), in1=tmp[:],
                                                             op0=ALU.mult, op1=ALU.add), reads=[tmpb_, Xb[t]], writes=[Xb[t]])

            def post_norm_all():
                sts = [ln_stats(X[:, t, :], Xb[t]) for t in range(4)]
                for t in range(4):
                    rstd, nmr, m_b = sts[t]
                    S.op("act", lambda e, t=t, rstd=rstd, nmr=nmr: e.activation(out=X[:, t, :], in_=X[:, t, :], func=AF.Identity, scale=rstd, bias=nmr),
                         reads=[Xb[t], m_b], writes=[Xb[t]])
                for t in range(4):
                    S.op("dve", lambda e, t=t: e.tensor_tensor(out=X[:, t, :], in0=X[:, t, :], in1=LG[:], op=ALU.mult), reads=[Xb[t], LGb], writes=[Xb[t]])
                    S.op("pool", lambda e, t=t: e.tensor_tensor(out=X[:, t, :], in0=X[:, t, :], in1=LB[:], op=ALU.add), reads=[Xb[t], LBb], writes=[Xb[t]])

            def attn_unit(uidx, q_ap, q_bufs, keys, sink_h0, o_dst, o_bufs):
                ob, obb = (att[0], attb[0]) if uidx % 2 == 0 else (mm[2], mmb[2])
                db, dbb = (att[1], attb[1]) if uidx % 2 == 0 else (mm[3], mmb[3])
                n = len(keys)
                pts = {}

                def issue_S(i):
                    si = nxt("mm", 2)
                    sbk, sbb = mm[si], mmb[si]
                    ky = keys[i]
                    S.group("pe", [lambda e: e.matmul(sbk[:], lhsT=ky["kT"], rhs=q_ap, start=True, stop=True)],
                            reads=q_bufs + ky["bufs"], writes=[sbb])
                    pt, ptb = next_pT()
                    S.op("act", lambda e: e.activation(out=pt[:], in_=sbk[:], func=AF.Exp, scale=SCALE), reads=[sbb], writes=[ptb])
                    if ky.get("mask") is not None:
                        mk = ky["mask"].unsqueeze(1).broadcast_to([128, 4, 128])
                        pv = pt[:].rearrange("p (h q) -> p h q", h=4)
                        S.op("dve", lambda e: e.tensor_tensor(out=pv, in0=pv, in1=mk, op=ALU.mult), reads=[ptb, constb], writes=[ptb])
                    pts[i] = (pt, ptb)

                def issue_PV(i):
                    pt, ptb = pts.pop(i)
                    ky = keys[i]
                    S.group("pe", [
                        lambda e: e.matmul(ob[:], lhsT=ky["v"], rhs=pt[:], start=(i == 0), stop=(i == n - 1)),
                        lambda e: e.matmul(db[:], lhsT=ones_bf[:], rhs=pt[:], start=(i == 0), stop=(i == n - 1)),
                    ], reads=[ptb, constb] + ky["bufs"], writes=[obb, dbb])

                issue_S(0)
                if n > 1:
                    issue_S(1)
                for i in range(n):
                    issue_PV(i)
                    if i + 2 < n:
                        issue_S(i + 2)
                rec, recb = next_ft()
                if sink_h0 is not None:
                    for h in range(4):
                        S.op("dve", lambda e, h=h: e.tensor_scalar(out=rec[:, h * 128:(h + 1) * 128], in0=db[:, h * 128:(h + 1) * 128],
                                                                   scalar1=esink[:, sink_h0 + h:sink_h0 + h + 1], scalar2=None, op0=ALU.add),
                             reads=[dbb, esinkb], writes=[recb])
                    S.op("dve", lambda e: e.reciprocal(out=rec[:], in_=rec[:]), reads=[recb], writes=[recb])
                else:
                    S.op("dve", lambda e: e.reciprocal(out=rec[:], in_=db[:]), reads=[dbb], writes=[recb])
                S.op("dve", lambda e: e.tensor_tensor(out=o_dst, in0=ob[:].rearrange("p (h q) -> p h q", h=4),
                                                      in1=rec[:].rearrange("p (h q) -> p h q", h=4), op=ALU.mult),
                     reads=[obb, recb], writes=o_bufs)

            def group_layer(gi, l, lat):
                sidx = 1 if lat else 0
                last = (l == depth - 1)
                xsrc = xl if lat else xc
                ydst = yl if lat else yc
                row0 = 0 if lat else gi * 512
                tag = lambda p: ("gl", gi, l, p)

                load_layer_consts(l, sidx, lat)
                for t in range(4):
                    if l == 0:
                        S.dma("sp", ds_x[t], [lambda e, t=t: e.dma_start(out=X[:, t, :], in_=xsrc[row0 + t * 128:row0 + (t + 1) * 128, :])],
                              writes=[Xb[t]])
                ln_to_hT_all(0, 16)
                check_stop(tag("P1"))

                S.fence(qTb + [GBb, Hb], aTb)
                pending = []

                def flush_pending():
                    while pending:
                        pending.pop(0)()

                for cbi in QKV_ORDER:
                    wt, wb_ = w_next(("in", l, cbi))
                    for t in range(4):
                        bank, bb = next_mm()
                        S.group("pe", [
                            (lambda e, k=k, bank=bank, wt=wt, t=t: e.matmul(bank[:], lhsT=hT[:, k, t * 128:(t + 1) * 128],
                                                                           rhs=wt[:, k, :], start=(k == 0), stop=(k == 15)))
                            for k in range(16)], reads=[hTb[t], wb_], writes=[bb])
                        flush_pending()
                        tsl = slice(t * 128, (t + 1) * 128)
                        is_q = cbi in (0, 1, 3, 4)
                        is_b = cbi >= 3
                        if is_q:
                            h0 = {0: 0, 1: 4, 3: 8, 4: 12}[cbi]
                            tb_, tbb = next_tmpb()
                            if not is_b:
                                if lat:
                                    rope(bank[:], [bb], 4, t, tb_[:], [tbb])
                                else:
                                    S.op("act", lambda e, tb_=tb_, bank=bank: e.activation(out=tb_[:], in_=bank[:], func=AF.Copy),
                                         reads=[bb], writes=[tbb])
                            else:
                                if lat:
                                    nr, nrb = next_ft()
                                    rms_norm(bank, bb, 4, gq_b, gqb, nr[:], [nrb])
                                    rope(nr[:], [nrb], 4, t, tb_[:], [tbb])
                                else:
                                    rms_norm(bank, bb, 4, gq_b, gqb, tb_[:], [tbb])
                            pending.append(lambda tb_=tb_, tbb=tbb, h0=h0, tsl=tsl, t=t: transpose_block(
                                tb_, tbb, 4, qT_v[:, h0:h0 + 4, tsl], [qTb[t]], eng="act"))
                        else:
                            kvh0 = 2 if is_b else 0
                            ncx = ncb if is_b else nca
                            tb_, tbb = next_tmpb()
                            S.op("act", lambda e, bank=bank, kvh0=kvh0, t=t: e.activation(
                                out=vv[:, kvh0:kvh0 + 2, t, :], in_=bank[:, 256:512].rearrange("p (h d) -> p h d", h=2), func=AF.Copy),
                                reads=[bb], writes=[vb[t]])
                            if lat:
                                if is_b:
                                    nr, nrb = next_ft()
                                    rms_norm(bank, bb, 2, gk_b, gkb, nr[:, 0:256], [nrb])
                                    rope(nr[:, 0:256], [nrb], 2, t, tb_[:, 0:256], [tbb])
                                else:
                                    rope(bank[:, 0:256], [bb], 2, t, tb_[:, 0:256], [tbb])
                            else:
                                kvo, kvob = next_ft()
                                if is_b:
                                    rms_norm(bank, bb, 2, gk_b, gkb, kvo[:, 0:256], [kvob])
                                    S.op("act", lambda e, kvo=kvo, bank=bank: e.activation(out=kvo[:, 256:512], in_=bank[:, 256:512], func=AF.Copy),
                                         reads=[bb], writes=[kvob])
                                else:
                                    S.op("act", lambda e, kvo=kvo, bank=bank: e.activation(out=kvo[:], in_=bank[:], func=AF.Copy),
                                         reads=[bb], writes=[kvob])
                                sq_ = gi * 2 + t // 2
                                r0 = (t % 2) * 128
                                out_toks.append(S.dma("sp", ds_fo[ft_idx[id(kvo)]], [
                                    lambda e, kvo=kvo, ncx=ncx, sq_=sq_, r0=r0: e.dma_start(out=ncx[sq_, l, 0, r0:r0 + 128, :], in_=kvo[:, 0:256]),
                                    lambda e, kvo=kvo, ncx=ncx, sq_=sq_, r0=r0: e.dma_start(out=ncx[sq_, l, 1, r0:r0 + 128, :], in_=kvo[:, 256:512]),
                                ], reads=[kvob], writes=[outb]))
                                S.op("dve", lambda e, kvo=kvo, tb_=tb_: e.tensor_copy(out=tb_[:, 0:256], in_=kvo[:, 0:256]),
                                     reads=[kvob], writes=[tbb])
                            pending.append(lambda tb_=tb_, tbb=tbb, kvh0=kvh0, tsl=tsl, t=t: transpose_block(
                                tb_, tbb, 2, kT[:, kvh0:kvh0 + 2, tsl], [kTb[t]], eng="act"))
                    if lat and cbi == 5:
                        flush_pending()
                    if lat and cbi == 5:
                        S.dma("sp", ds_kvst, [
                            lambda e: e.dma_start(out=kvst[:, 0:2048], in_=kT[:].rearrange("p h t -> p (h t)")),
                            lambda e: e.dma_start(out=kvst[:, 2048:4096], in_=vv[:].rearrange("p h t d -> p (h t d)")),
                        ], reads=kTb + vb, writes=[kvstb])
                        ctr["cc"] += 1
                        S.custom("pool", lambda e, v=ctr["cc"]: e.collective_compute(
                            "AllGather", ALU.bypass, replica_groups=RG, ins=[kvst_t.ap().opt()], outs=[kvg_t.ap().opt()]).then_inc(ccsem, 1),
                            ccsem, ctr["cc"], reads=[kvstb], writes=[kvgb])
                        kvg_v = kvg.rearrange("(r p) x -> p r x", p=128)
                        S.dma("sp", ds_gb, [
                            lambda e: e.dma_start(out=GB_v[:, :, 0:1024], in_=kvg_v[:, :, 1024:2048]),
                            lambda e: e.dma_start(out=GB_v[:, :, 1024:2048], in_=kvg_v[:, :, 3072:4096]),
                        ], reads=[kvgb], writes=[GBb])
                        kA = kvg_v[:, :, 0:1024].rearrange("p r (h t) -> p r h t", h=2)
                        vA = kvg_v[:, :, 2048:3072].rearrange("p r (h tt d) -> p r h tt d", h=2, tt=4)
                        hfns = []
                        for h_ in range(2):
                            hfns += [
                                lambda e, h_=h_: e.dma_start(out=H_v[:, :, 0 + h_ * 128:0 + (h_ + 1) * 128], in_=kA[:, :, h_, 384:512]),
                                lambda e, h_=h_: e.dma_start(out=H_v[:, :, 256 + h_ * 128:256 + (h_ + 1) * 128], in_=vA[:, :, h_, 3, :]),
                                lambda e, h_=h_: e.dma_start(out=H_v[:, :, 512 + h_ * 128:512 + (h_ + 1) * 128], in_=kA[:, :, h_, 0:128]),
                                lambda e, h_=h_: e.dma_start(out=H_v[:, :, 768 + h_ * 128:768 + (h_ + 1) * 128], in_=vA[:, :, h_, 0, :]),
                            ]
                        S.dma("sp", ds_h, hfns, reads=[kvgb], writes=[Hb])
                        for side in range(2):
                            hs = HS[:, side * 512:(side + 1) * 512]
                            for r in range(4):
                                src = H_v[:, r, side * 512:(side + 1) * 512]
                                sc_ = sel[:, side * 4 + r:side * 4 + r + 1]
                                if r == 0:
                                    S.op("dve", lambda e, hs=hs, src=src, sc_=sc_: e.tensor_scalar(out=hs, in0=src, scalar1=sc_, scalar2=None, op0=ALU.mult),
                                         reads=[Hb, constb], writes=[HSb])
                                else:
                                    S.op("dve", lambda e, hs=hs, src=src, sc_=sc_: e.scalar_tensor_tensor(out=hs, in0=src, scalar=sc_, in1=hs, op0=ALU.mult, op1=ALU.add),
                                         reads=[Hb, constb, HSb], writes=[HSb])
                flush_pending()
                check_stop(tag("P2"))

                u = 0
                if not lat:
                    for s_ in range(2):
                        for g in range(4):
                            for qt in range(2):
                                t = s_ * 2 + qt
                                keys = [dict(kT=kT[:, g, (s_ * 2 + kt) * 128:(s_ * 2 + kt + 1) * 128], v=vv[:, g, s_ * 2 + kt, :],
                                             bufs=[kTb[s_ * 2 + kt], vb[s_ * 2 + kt]]) for kt in range(2)]
                                attn_unit(u, qT_v[:, 4 * g:4 * g + 4, t * 128:(t + 1) * 128], [qTb[t]], keys,
                                          (4 * g if g < 2 else None), hT[:, 4 * g:4 * g + 4, t * 128:(t + 1) * 128], [hTb[t]])
                                u += 1
                else:
                    triP, triN, mL, mR = (masks[:, i * 128:(i + 1) * 128] for i in range(4))
                    unit_order = ([(g, qt) for g in (0, 1) for qt in (1, 2)] + [(g, qt) for g in (0, 1) for qt in (0, 3)]
                                  + [(g, qt) for g in (2, 3) for qt in range(4)])
                    for (g, qt) in unit_order:
                        if True:
                            keys = []
                            if g < 2:
                                if qt == 0:
                                    keys.append(dict(kT=HS[:, g * 128:(g + 1) * 128], v=HS[:, 256 + g * 128:256 + (g + 1) * 128], mask=mL, bufs=[HSb]))
                                else:
                                    keys.append(dict(kT=kT[:, g, (qt - 1) * 128:qt * 128], v=vv[:, g, qt - 1, :], mask=triP, bufs=[kTb[qt - 1], vb[qt - 1]]))
                                keys.append(dict(kT=kT[:, g, qt * 128:(qt + 1) * 128], v=vv[:, g, qt, :], bufs=[kTb[qt], vb[qt]]))
                                if qt == 3:
                                    keys.append(dict(kT=HS[:, 512 + g * 128:512 + (g + 1) * 128], v=HS[:, 768 + g * 128:768 + (g + 1) * 128], mask=mR, bufs=[HSb]))
                                else:
                                    keys.append(dict(kT=kT[:, g, (qt + 1) * 128:(qt + 2) * 128], v=vv[:, g, qt + 1, :], mask=triN, bufs=[kTb[qt + 1], vb[qt + 1]]))
                            else:
                                hB = g - 2
                                for r in range(4):
                                    for tt in range(4):
                                        keys.append(dict(kT=GB_v[:, r, hB * 512 + tt * 128:hB * 512 + (tt + 1) * 128],
                                                         v=GB_v[:, r, 1024 + hB * 512 + tt * 128:1024 + hB * 512 + (tt + 1) * 128], bufs=[GBb]))
                            for tt in range(2):
                                keys.append(dict(kT=ckT[:, g, tt * 128:(tt + 1) * 128], v=cv_sb[:, tt, g // 2, (g % 2) * 128:(g % 2 + 1) * 128],
                                                 bufs=[ckTb, cvb]))
                            attn_unit(u, qT_v[:, 4 * g:4 * g + 4, qt * 128:(qt + 1) * 128], [qTb[qt]], keys,
                                      (4 * g if g < 2 else None), hT[:, 4 * g:4 * g + 4, qt * 128:(qt + 1) * 128], [hTb[qt]])
                            u += 1
                check_stop(tag("P3"))

                load_bcast(l, sidx, 1)
                for cb in range(4):
                    wt, wb_ = w_next(("o", l, cb))
                    for t in range(4):
                        bank, bb = next_mm()
                        S.group("pe", [
                            (lambda e, k=k, bank=bank, wt=wt, t=t: e.matmul(bank[:], lhsT=hT[:, k, t * 128:(t + 1) * 128],
                                                                           rhs=wt[:, k, :], start=(k == 0), stop=(k == 15)))
                            for k in range(16)], reads=[hTb[t], wb_], writes=[bb])
                        resid_update(bank, bb, t, cb)
                post_norm_all()
                ln_to_hT_all(48, 64)
                check_stop(tag("P4"))

                S.fence(aTb, qTb + [GBb, Hb])
                valc = [XH[:, i * 512:(i + 1) * 512] for i in range(4)]
                valb = [Buf(f"valc{i}") for i in range(4)]
                S.fence(valb, [XHb])
                nseg = 1 if lat else 2
                seglen = 512 // nseg

                def conv(bank, bb, jj, dst, dstb):
                    w0 = cw[:, jj * 4 + 0:jj * 4 + 1]
                    w1 = cw[:, jj * 4 + 1:jj * 4 + 2]
                    w2 = cw[:, jj * 4 + 2:jj * 4 + 3]
                    b_ = cw[:, jj * 4 + 3:jj * 4 + 4]
                    S.op("act", lambda e: e.activation(out=dst, in_=bank[:], func=AF.Identity, scale=w1, bias=b_),
                         reads=[bb, cwb], writes=[dstb])
                    bv = bank[:].rearrange("p (s t) -> p s t", s=nseg)
                    dv = dst.rearrange("p (s t) -> p s t", s=nseg)
                    S.op("dve", lambda e: e.scalar_tensor_tensor(out=dv[:, :, 1:seglen], in0=bv[:, :, 0:seglen - 1], scalar=w0,
                                                                 in1=dv[:, :, 1:seglen], op0=ALU.mult, op1=ALU.add),
                         reads=[bb, cwb, dstb], writes=[dstb])
                    S.op("dve", lambda e: e.scalar_tensor_tensor(out=dv[:, :, 0:seglen - 1], in0=bv[:, :, 1:seglen], scalar=w2,
                                                                 in1=dv[:, :, 0:seglen - 1], op0=ALU.mult, op1=ALU.add),
                         reads=[bb, cwb, dstb], writes=[dstb])
                    if lat:
                        S.op("act", lambda e: e.activation(out=E_own[:, jj, :], in_=bank[:, 0:512:511], func=AF.Copy), reads=[bb], writes=[Eb])
                        S.op("act", lambda e: e.activation(out=P_edge[:, jj, :], in_=dst[:, 0:512:511], func=AF.Copy), reads=[dstb], writes=[Pb])

                for c in range(11):
                    wt, wb_ = w_next(("up", l, c))
                    for f4 in range(4):
                        bank, bb = next_mm()
                        S.group("pe", [
                            (lambda e, k=k, bank=bank, wt=wt, f4=f4: e.matmul(bank[:], lhsT=wt[:, k, f4 * 128:(f4 + 1) * 128],
                                                                             rhs=hT[:, k, :], start=(k == 0), stop=(k == 15)))
                            for k in range(16)], reads=hTb + [wb_], writes=[bb])
                        conv(bank, bb, c * 4 + f4, valc[f4], valb[f4])
                    wt, wb_ = w_next(("up", l, 11 + c))
                    for f4 in range(4):
                        j = c * 4 + f4
                        bank, bb = next_mm()
                        S.group("pe", [
                            (lambda e, k=k, bank=bank, wt=wt, f4=f4: e.matmul(bank[:], lhsT=wt[:, k, f4 * 128:(f4 + 1) * 128],
                                                                             rhs=hT[:, k, :], start=(k == 0), stop=(k == 15)))
                            for k in range(16)], reads=hTb + [wb_], writes=[bb])
                        gc, gcb = next_ft()
                        conv(bank, bb, NFT + j, gc[:], gcb)
                        S.op("act", lambda e, gc=gc: e.activation(out=gc[:], in_=gc[:], func=AF.Silu), reads=[gcb], writes=[gcb])
                        S.op("dve", lambda e, gc=gc, f4=f4, j=j: e.tensor_tensor(out=aT_v[:, j, :], in0=gc[:], in1=valc[f4], op=ALU.mult),
                             reads=[gcb, valb[f4]], writes=[aTb[j]])
                S.fence([XHb], valb)
                if lat:
                    S.dma("sp", ds_est, [lambda e: e.dma_start(out=est[:, :], in_=E_own[:].rearrange("p j e -> p (j e)"))],
                          reads=[Eb], writes=[estb])
                    ctr["cc"] += 1
                    S.custom("pool", lambda e: e.collective_compute(
                        "AllGather", ALU.bypass, replica_groups=RG, ins=[est_t.ap().opt()], outs=[eg_t.ap().opt()]).then_inc(ccsem, 1),
                        ccsem, ctr["cc"], reads=[estb], writes=[egb])
                    S.dma("sp", ds_g2, [lambda e: e.dma_start(out=G2[:], in_=eg.rearrange("(r p) x -> p r x", p=128))],
                          reads=[egb], writes=[G2b])
                    g2v = G2[:].rearrange("p r (j e) -> p r j e", e=2)
                    for side in range(2):
                        edge = 1 - side
                        dst = hlr[:, side, :]
                        for r in range(4):
                            src = g2v[:, r, :, edge]
                            sc_ = sel[:, side * 4 + r:side * 4 + r + 1]
                            if r == 0:
                                S.op("dve", lambda e, dst=dst, src=src, sc_=sc_: e.tensor_scalar(out=dst, in0=src, scalar1=sc_, scalar2=None, op0=ALU.mult),
                                     reads=[G2b, constb], writes=[hlrb])
                            else:
                                S.op("dve", lambda e, dst=dst, src=src, sc_=sc_: e.scalar_tensor_tensor(out=dst, in0=src, scalar=sc_, in1=dst, op0=ALU.mult, op1=ALU.add),
                                     reads=[G2b, constb, hlrb], writes=[hlrb])
                    cwv = cw[:].rearrange("p (j q) -> p j q", q=4)
                    for side in range(2):
                        wsel = cwv[:, :, 0] if side == 0 else cwv[:, :, 2]
                        S.op("dve", lambda e, side=side, wsel=wsel: e.tensor_tensor(out=etmp[:], in0=hlr[:, side, :], in1=wsel, op=ALU.mult),
                             reads=[hlrb, cwb], writes=[etmpb])
                        S.op("dve", lambda e, side=side: e.tensor_tensor(out=P_edge[:, :, side], in0=P_edge[:, :, side], in1=etmp[:], op=ALU.add),
                             reads=[etmpb, Pb], writes=[Pb])
                    pg = P_edge[:, NFT:2 * NFT, :]
                    S.op("act", lambda e: e.activation(out=pg, in_=pg, func=AF.Silu), reads=[Pb], writes=[Pb])
                    S.op("dve", lambda e: e.tensor_tensor(out=aT_v[:, :, 0:512:511], in0=pg, in1=P_edge[:, 0:NFT, :], op=ALU.mult),
                         reads=[Pb], writes=aTb)
                check_stop(tag("P5"))

                load_bcast(l, sidx, 2)
                ada_n = [0]
                for cb in range(4):
                    for sub in range(3):
                        wt, wb_ = w_next(("down", l, cb, sub))
                        k0 = sub * 16
                        nk = min(16, NFT - k0)
                        for t in range(4):
                            S.group("pe", [
                                (lambda e, kk=kk, t=t, wt=wt, k0=k0, nk=nk, sub=sub: e.matmul(
                                    mm[t][:], lhsT=aT_v[:, k0 + kk, t * 128:(t + 1) * 128], rhs=wt[:, kk, :],
                                    start=(sub == 0 and kk == 0), stop=(sub == 2 and kk == nk - 1)))
                                for kk in range(nk)], reads=aTb[k0:k0 + nk] + [wb_], writes=[mmb[t]])
                            if sub == 2:
                                resid_update(mm[t], mmb[t], t, cb)
                        if gi == groups[0] and l == 0 and depth > 1:
                            if ada_n[0] == 0:
                                S.fence(rowb + biasb, [XHb])
                            for (i2, cb2) in ada_l1_order()[ada_n[0]:ada_n[0] + 2]:
                                ai = ada_n[0] % 2
                                ada_block(1, i2, cb2, att[ai], attb[ai])
                                ada_n[0] += 1
                if gi == groups[0] and l == 0 and depth > 1:
                    S.fence([XHb], rowb + biasb)
                post_norm_all()
                for t in range(4):
                    if last:
                        out_toks.append(S.dma("sp", ds_xo[t], [lambda e, t=t: e.dma_start(
                            out=ydst[row0 + t * 128:row0 + (t + 1) * 128, :], in_=X[:, t, :])], reads=[Xb[t]], writes=[outb]))
                check_stop(tag("P6"))

            for gi in groups:
                for l in range(depth):
                    group_layer(gi, l, gi == 2)
        except StopBuild:
            pass

        dbg_map = {
            "X": (X, [128, 4 * D], F32, "p t d -> p (t d)", Xb),
            "hT": (hT, [128, 16 * 512], BF16, "p k t -> p (k t)", hTb),
            "R": (R, [128, 24576], BF16, None, qTb + [GBb, Hb] + aTb),
            "kT": (kT, [128, 2048], BF16, "p h t -> p (h t)", kTb),
            "vv": (vv, [128, 2048], BF16, "p h t d -> p (h t d)", vb),
            "modT": (modT, [128, 96], F32, None, [modTb]),
            "HS": (HS, [128, 1024], BF16, None, [HSb]),
        }
        for name in dbg:
            ten, shp, dt, rr, bufs = dbg_map[name]
            d_ap = dout("dbg_" + name, shp, dt)
            src = ten[:] if rr is None else ten[:].rearrange(rr)
            out_toks.append(S.dma("sp", ds_dbg, [lambda e, d_ap=d_ap, src=src: e.dma_start(out=d_ap[:, :], in_=src)], reads=bufs))

        for d in S.all_dsems:
            if d.count > 0:
                S.final_wait("sp", [(d.sem, d.count, "dma")])
        S.final_wait("sp", out_toks)
        with nc.Block() as block:
            S.emit(block)
    return nc


def _rope_tables(j):
    n = np.arange(512 * j, 512 * (j + 1))
    row = (n // GRID_W).astype(np.float32)
    col = (n % GRID_W).astype(np.float32)
    q4 = HD // 4
    freq = (ROPE_THETA ** (-np.arange(q4, dtype=np.float32) / q4)).astype(np.float32)
    ang = np.stack([row[:, None] * freq, col[:, None] * freq], axis=1).astype(np.float32)
    c = np.cos(ang).astype(np.float32).reshape(4, 128, 64).transpose(1, 0, 2).reshape(128, 256)
    s = np.sin(ang).astype(np.float32).reshape(4, 128, 64).transpose(1, 0, 2).reshape(128, 256)
    return np.ascontiguousarray(c), np.ascontiguousarray(s)


def _masks(j):
    s = np.arange(128)[:, None]
    q = np.arange(128)[None, :]
    triP = (q <= s).astype(np.float32)
    triN = (s <= q).astype(np.float32)
    mL = triP if j > 0 else np.zeros_like(triP)
    mR = triN if j < 3 else np.zeros_like(triN)
    return np.ascontiguousarray(np.concatenate([triP, triN, mL, mR], axis=1))


def make_in_maps(inp):
    f = lambda a: np.ascontiguousarray(np.asarray(a, dtype=np.float32))
    x_prompt, x_sample = f(inp["x_prompt"]), f(inp["x_sample"])
    ca, cb_ = f(inp["cache_attn_a"]), f(inp["cache_attn_b"])
    c, c_ctx = f(inp["c"]), f(inp["c_ctx"])
    def blk(w):
        l_, k_, n_ = w.shape
        return np.ascontiguousarray(w.reshape(l_, k_ // 128, 128, n_ // 512, 512).transpose(0, 3, 2, 1, 4).reshape(l_, n_ // 512, 128, (k_ // 128) * 512))

    shared = dict(
        w_ada=blk(f(inp["w_ada"])), b_ada=f(inp["b_ada"]), w_in=blk(f(inp["w_in"])), w_o=blk(f(inp["w_o"])),
        w_up=blk(f(inp["w_up"])), w_down=blk(f(inp["w_down"])), qg=f(inp["q_norm_g"]), kg=f(inp["k_norm_g"]),
        sink=f(inp["sink_a"]), ln1g=f(inp["ln1_g"]), ln1b=f(inp["ln1_b"]), ln2g=f(inp["ln2_g"]), ln2b=f(inp["ln2_b"]),
        ident=np.eye(128, dtype=np.float32),
    )
    conv_w, conv_b = f(inp["conv_w"]), f(inp["conv_b"])
    cwp = np.concatenate([conv_w, conv_b[:, None, :]], axis=1)
    cwp = cwp.reshape(L, 4, 88, 128).transpose(0, 3, 2, 1).reshape(L, 128, 352)
    shared["cw"] = np.ascontiguousarray(cwp)
    maps = []
    for i in range(NCORES):
        b, j = i // 4, i % 4
        m = dict(shared)
        m["xc"] = np.ascontiguousarray(x_prompt[4 * i:4 * i + 4].reshape(1024, D))
        m["xl"] = np.ascontiguousarray(x_sample[b, 512 * j:512 * (j + 1)])
        cc = np.stack([ca[b].reshape(L, 2, 256, 256), cb_[b].reshape(L, 2, 256, 256)], axis=1)
        m["cch"] = np.ascontiguousarray(cc)
        ct = np.zeros((128, 16, 33), np.float32)
        ct[:, :, 0] = c_ctx.reshape(16, 128).T
        ct[:, :, 32] = c[b].reshape(16, 128).T
        m["cT"] = np.ascontiguousarray(ct.reshape(128, 16 * 33))
        rc, rs = _rope_tables(j)
        m["ropec"], m["ropes"] = rc, rs
        m["masks"] = _masks(j)
        sl = np.zeros((128, 8), np.float32)
        if j > 0:
            sl[:, j - 1] = 1.0
        if j < 3:
            sl[:, 4 + j + 1] = 1.0
        m["sel"] = sl
        maps.append(m)
    return maps


_NC_CACHE = {}


def kernel(**inputs):
    maps = make_in_maps(inputs)
    if "nc" not in _NC_CACHE:
        _NC_CACHE["nc"] = build_program()
    nc = _NC_CACHE["nc"]
    res = run_bass_kernel_spmd(nc, maps, core_ids=list(range(NCORES)))
    rs = res.results
    y_prompt = np.concatenate([np.asarray(r["yc"]).reshape(4, SEQ, D) for r in rs], axis=0).astype(np.float32)
    y_sample = np.stack([np.concatenate([np.asarray(rs[b * 4 + j]["yl"]) for j in range(4)], axis=0) for b in range(2)], axis=0).astype(np.float32)
    nca = np.concatenate([np.asarray(r["nca"]) for r in rs], axis=0).reshape(32, L, 2, SEQ, 2, HD).astype(np.float32)
    ncb = np.concatenate([np.asarray(r["ncb"]) for r in rs], axis=0).reshape(32, L, 2, SEQ, 2, HD).astype(np.float32)
    return (y_prompt, y_sample, nca, ncb)
```

```python
import math
from contextlib import ExitStack

import numpy as np
import concourse.bass as bass
import concourse.mybir as mybir
from concourse.bass_utils import run_bass_kernel_spmd

F32 = mybir.dt.float32
BF16 = mybir.dt.bfloat16
ALU = mybir.AluOpType
AF = mybir.ActivationFunctionType
AX = mybir.AxisListType

D = 2048
L = 2
NCORES = 8
SEQ = 256
DFF = 5632
NFT = DFF // 128
HD = 128
EPS = 1e-6
ALPHA = (2.0 * L) ** 0.25
SCALE = HD ** -0.5
GRID_W = 64
ROPE_THETA = 10000.0
QKV_ORDER = (2, 5, 0, 1, 3, 4)
QKV_COL = (0, 512, 1024, 1536, 2048, 2560)
RG = [[0, 1, 2, 3], [4, 5, 6, 7]]


class Buf:
    __slots__ = ("name", "last_w", "readers", "excl")

    def __init__(self, name, excl=False):
        self.name = name
        self.last_w = None
        self.readers = {}
        self.excl = excl


class EngQ:
    def __init__(self, name, sem):
        self.name = name
        self.sem = sem
        self.count = 0
        self.ops = []
        self.waited = {}


class DSem:
    def __init__(self, sem):
        self.sem = sem
        self.count = 0


class StopBuild(Exception):
    pass


class Sched:
    def __init__(self, nc, stack):
        self.nc = nc
        self.stack = stack
        self.q = {}
        for n in ("pe", "act", "dve", "pool", "sp"):
            self.q[n] = EngQ(n, stack.enter_context(nc.semaphore("sem_" + n)))
        self.n_dsem = 0
        self.all_dsems = []

    def dsem(self, name=None):
        self.n_dsem += 1
        d = DSem(self.stack.enter_context(self.nc.semaphore(name or f"dsem{self.n_dsem}")))
        self.all_dsems.append(d)
        return d

    def _wait(self, q, tok):
        if tok is None:
            return
        sem, val, owner = tok
        if owner == "pe" and q.name == "pe":
            return
        key = id(sem)
        if q.waited.get(key, 0) >= val:
            return
        q.waited[key] = val
        q.ops.append(lambda e, sem=sem, val=val: e.wait_ge(sem, val))

    def _deps(self, q, reads, writes):
        for b in reads:
            self._wait(q, b.last_w)
            if b.excl:
                for r in list(b.readers.values()):
                    if r[2] != q.name:
                        self._wait(q, r)
        for b in writes:
            self._wait(q, b.last_w)
            for r in list(b.readers.values()):
                self._wait(q, r)

    @staticmethod
    def _addr(d, tok):
        k = id(tok[0])
        if k not in d or d[k][1] < tok[1]:
            d[k] = tok

    def _commit(self, tok, reads, writes):
        for b in reads:
            self._addr(b.readers, tok)
        for b in writes:
            b.last_w = tok
            b.readers = {}

    def op(self, eng, fn, reads=(), writes=()):
        q = self.q[eng]
        self._deps(q, reads, writes)
        q.count += 1
        val = q.count
        sem = q.sem
        q.ops.append(lambda e, fn=fn, sem=sem: fn(e).then_inc(sem, 1))
        tok = (sem, val, eng)
        self._commit(tok, reads, writes)
        return tok

    def group(self, eng, fns, reads=(), writes=()):
        q = self.q[eng]
        self._deps(q, reads, writes)
        for fn in fns[:-1]:
            q.ops.append(lambda e, fn=fn: fn(e))
        q.count += 1
        val = q.count
        sem = q.sem
        q.ops.append(lambda e, fn=fns[-1], sem=sem: fn(e).then_inc(sem, 1))
        tok = (sem, val, eng)
        self._commit(tok, reads, writes)
        return tok

    def dma(self, eng, ds, fns, reads=(), writes=()):
        q = self.q[eng]
        self._deps(q, reads, writes)
        for fn in fns:
            ds.count += 16
            q.ops.append(lambda e, fn=fn, sem=ds.sem: fn(e).then_inc(sem, 16))
        tok = (ds.sem, ds.count, "dma")
        self._commit(tok, reads, writes)
        return tok

    def custom(self, eng, fn, sem, val, reads=(), writes=()):
        q = self.q[eng]
        self._deps(q, reads, writes)
        q.ops.append(lambda e, fn=fn: fn(e))
        tok = (sem, val, "custom")
        self._commit(tok, reads, writes)
        return tok

    def fence(self, dst_bufs, src_bufs):
        toks = []
        for s in src_bufs:
            if s.last_w is not None:
                toks.append(s.last_w)
            toks.extend(s.readers.values())
        for d in dst_bufs:
            for t in toks:
                self._addr(d.readers, t)

    def final_wait(self, eng, toks):
        q = self.q[eng]
        for t in toks:
            self._wait(q, t)

    def emit(self, block):
        qs = self.q

        @block.tensor
        def _(e):
            for f in qs["pe"].ops:
                f(e)

        @block.scalar
        def _(e):
            for f in qs["act"].ops:
                f(e)

        @block.vector
        def _(e):
            for f in qs["dve"].ops:
                f(e)

        @block.gpsimd
        def _(e):
            for f in qs["pool"].ops:
                f(e)

        @block.sync
        def _(e):
            for f in qs["sp"].ops:
                f(e)


def ada_l1_order():
    return [(i, cb) for i in range(6) for cb in range(4)]


def weight_sequence(depth, groups):
    seq = []
    for i in range(6):
        for cb in range(4):
            seq.append(("ada", 0, i, cb))
    for gn, gi in enumerate(groups):
        for l in range(depth):
            for cbi in QKV_ORDER:
                seq.append(("in", l, cbi))
            for cb in range(4):
                seq.append(("o", l, cb))
            for c in range(11):
                seq.append(("up", l, c))
                seq.append(("up", l, 11 + c))
            n = 0
            for cb in range(4):
                for sub in range(3):
                    seq.append(("down", l, cb, sub))
                    if gn == 0 and l == 0 and depth > 1:
                        for (i2, cb2) in ada_l1_order()[n:n + 2]:
                            seq.append(("ada", 1, i2, cb2))
                        n += 2
    return seq


def build_program(depth=L, groups=(0, 1, 2), stop=None, dbg=()):
    nc = bass.Bass("TRN2", target_bir_lowering=False)

    def din(n, shp, dt=F32):
        return nc.dram_tensor(n, shp, dt, kind="ExternalInput").ap()

    def dout(n, shp, dt=F32):
        return nc.dram_tensor(n, shp, dt, kind="ExternalOutput").ap()

    xc = din("xc", [1024, D])
    xl = din("xl", [512, D])
    cch = din("cch", [L, 2, 2, 256, 256])
    cT_d = din("cT", [128, 16 * 33])
    w_ada = din("w_ada", [L, 24, 128, 8192])
    b_ada = din("b_ada", [L, 6 * D])
    w_in = din("w_in", [L, 6, 128, 8192])
    w_o = din("w_o", [L, 4, 128, 8192])
    w_up = din("w_up", [L, 22, 128, 8192])
    w_down = din("w_down", [L, 4, 128, NFT * 512])
    qg_d = din("qg", [L, 128])
    kg_d = din("kg", [L, 128])
    sink_d = din("sink", [L, 8])
    ln1g_d = din("ln1g", [L, D])
    ln1b_d = din("ln1b", [L, D])
    ln2g_d = din("ln2g", [L, D])
    ln2b_d = din("ln2b", [L, D])
    cw_d = din("cw", [L, 128, 4 * 88])
    ropec_d = din("ropec", [128, 256])
    ropes_d = din("ropes", [128, 256])
    masks_d = din("masks", [128, 512])
    sel_d = din("sel", [128, 8])
    ident_d = din("ident", [128, 128])

    yc = dout("yc", [1024, D])
    yl = dout("yl", [512, D])
    nca = dout("nca", [4, L, 2, 256, 256])
    ncb = dout("ncb", [4, L, 2, 256, 256])

    modd_t = nc.dram_tensor("modd", [L, 2, 6 * D], F32)
    kvst_t = nc.dram_tensor("kvst", [128, 4096], BF16)
    kvg_t = nc.dram_tensor("kvg", [512, 4096], BF16)
    est_t = nc.dram_tensor("est", [128, 176], F32)
    eg_t = nc.dram_tensor("eg", [512, 176], F32)
    modd = modd_t.ap()
    kvst = kvst_t.ap()
    kvg = kvg_t.ap()
    est = est_t.ap()
    eg = eg_t.ap()

    dbg_outs = {}

    with ExitStack() as st:
        S = Sched(nc, st)

        def sb(n, shp, dt):
            return st.enter_context(nc.sbuf_tensor("s_" + n, shp, dt))

        def ps(n, shp, dt):
            return st.enter_context(nc.psum_tensor("p_" + n, shp, dt))

        X = sb("X", [128, 4, D], F32)
        XH = sb("XH", [128, D], F32)
        hT = sb("hT", [128, 16, 512], BF16)
        R = sb("R", [128, 24576], BF16)
        kT = sb("kT", [128, 4, 512], BF16)
        vv = sb("vv", [128, 4, 4, 128], BF16)
        wr = [sb(f"wr{i}", [128, 16, 512], BF16) for i in range(2)]
        GT = sb("GT", [128, D], F32)
        LG = sb("LG", [128, D], F32)
        LB = sb("LB", [128, D], F32)
        idf = sb("idf", [128, 128], F32)
        idb = sb("idb", [128, 128], BF16)
        ones_bf = sb("ones_bf", [128, 128], BF16)
        masks = sb("masks", [128, 512], BF16)
        ropec = sb("ropec", [128, 256], F32)
        ropes = sb("ropes", [128, 256], F32)
        sel = sb("sel", [128, 8], F32)
        mod_in = sb("mod_in", [96, 128], F32)
        modT = sb("modT", [128, 96], F32)
        cw = sb("cw", [128, 4 * 88], F32)
        gq_b = sb("gq_b", [128, 128], F32)
        gk_b = sb("gk_b", [128, 128], F32)
        esink = sb("esink", [128, 8], F32)
        cv_sb = sb("cv_sb", [128, 2, 2, 256], BF16)
        ck_sb = sb("ck_sb", [128, 2, 2, 256], BF16)
        ckT = sb("ckT", [128, 4, 256], BF16)
        cT = sb("cT", [128, 16 * 33], F32)
        sT = sb("sT", [128, 16, 33], BF16)
        tmpb = [sb(f"tmpb{i}", [128, 512], BF16) for i in range(2)]
        ft = [sb(f"ft{i}", [128, 512], F32) for i in range(4)]
        pT = [sb(f"pT{i}", [128, 512], BF16) for i in range(3)]
        HS = sb("HS", [128, 1024], BF16)
        E_own = sb("E_own", [128, 88, 2], F32)
        P_edge = sb("P_edge", [128, 88, 2], F32)
        G2 = sb("G2", [128, 4, 176], F32)
        hlr = sb("hlr", [128, 2, 88], F32)
        etmp = sb("etmp", [128, 88], F32)
        stt = [sb(f"stt{i}", [128, 4, 6], F32) for i in range(4)]
        mv = [sb(f"mv{i}", [128, 8], F32) for i in range(4)]
        ss = sb("ss", [128, 16], F32)
        sttA = [sb(f"sttA{i}", [128, 4, 4, 6], F32) for i in range(2)]
        mvA = [sb(f"mvA{i}", [128, 4, 8], F32) for i in range(2)]

        mm = [ps(f"mm{i}", [128, 512], F32) for i in range(4)]
        tr = [ps(f"tr{i}", [128, 1024], BF16) for i in range(2)]
        att = [ps(f"att{i}", [128, 512], F32) for i in range(2)]

        qT_v = R[:, 0:8192].rearrange("p (h t) -> p h t", h=16)
        GB_v = R[:, 8192:16384].rearrange("p (r x) -> p r x", r=4)
        H_v = R[:, 16384:20480].rearrange("p (r x) -> p r x", r=4)
        aT_v = R[:, 0:22528].rearrange("p (j t) -> p j t", j=NFT)

        Xb = [Buf(f"X{t}") for t in range(4)]
        XHb = Buf("XH")
        hTb = [Buf(f"hT{t}") for t in range(4)]
        qTb = [Buf(f"qT{t}") for t in range(4)]
        kTb = [Buf(f"kT{t}") for t in range(4)]
        vb = [Buf(f"v{t}") for t in range(4)]
        GBb = Buf("GB")
        Hb = Buf("H")
        aTb = [Buf(f"aT{j}") for j in range(NFT)]
        wrb = [Buf("wr0"), Buf("wr1")]
        GTb, LGb, LBb = Buf("GT"), Buf("LG"), Buf("LB")
        constb = Buf("const")
        mod_inb, modTb, cwb, gqb, gkb, esinkb = Buf("mod_in"), Buf("modT"), Buf("cw"), Buf("gq"), Buf("gk"), Buf("esink")
        auxb = Buf("aux")
        cvb = ckb = auxb
        ckTb = Buf("ckT")
        cTb, sTb = Buf("cT"), Buf("sT")
        tmpbb = [Buf("tmpb0"), Buf("tmpb1")]
        ftb = [Buf(f"ft{i}") for i in range(4)]
        pTb = [Buf(f"pT{i}") for i in range(3)]
        HSb, Eb, Pb, G2b, hlrb, etmpb = Buf("HS"), Buf("E"), Buf("P"), Buf("G2"), Buf("hlr"), Buf("etmp")
        sttb = [Buf(f"stt{i}") for i in range(4)]
        mvb = [Buf(f"mv{i}") for i in range(4)]
        ssb = Buf("ss")
        sttAb = [Buf("sttA0"), Buf("sttA1")]
        mvAb = [Buf("mvA0"), Buf("mvA1")]
        mmb = [Buf(f"mm{i}", excl=True) for i in range(4)]
        trb = [Buf(f"tr{i}", excl=True) for i in range(2)]
        attb = [Buf(f"att{i}", excl=True) for i in range(2)]
        moddb2 = [Buf("modd0"), Buf("modd1")]
        kvstb, kvgb, estb, egb = Buf("kvst"), Buf("kvg"), Buf("est"), Buf("eg")
        outb = Buf("out")

        ctr = {"mm": 0, "tr": 0, "ft": 0, "tmpb": 0, "pT": 0, "stat": 0, "cc": 0, "nrow": 0, "statA": 0}

        def nxt(key, n):
            i = ctr[key] % n
            ctr[key] += 1
            return i

        def next_mm():
            i = nxt("mm", 4)
            return mm[i], mmb[i]

        def next_tr():
            i = nxt("tr", 2)
            return tr[i], trb[i]

        ft_idx = {}

        def next_ft():
            i = nxt("ft", 4)
            ft_idx[id(ft[i])] = i
            return ft[i], ftb[i]

        def next_tmpb():
            i = nxt("tmpb", 2)
            return tmpb[i], tmpbb[i]

        def next_pT():
            i = nxt("pT", 3)
            return pT[i], pTb[i]

        out_toks = []
        ccsem = st.enter_context(nc.semaphore("ccsem"))

        ds_x = [S.dsem(f"ds_x{t}") for t in range(4)]
        ds_wr = [[S.dsem(f"ds_wr{i}_{j}") for j in range(2)] for i in range(2)]
        ds_c = S.dsem("ds_const")
        ds_gt, ds_lg, ds_lb = S.dsem("ds_gt"), S.dsem("ds_lg"), S.dsem("ds_lb")
        ds_mod, ds_cw, ds_gq, ds_gk, ds_sink = S.dsem("ds_mod"), S.dsem("ds_cw"), S.dsem("ds_gq"), S.dsem("ds_gk"), S.dsem("ds_sink")
        ds_aux = S.dsem("ds_aux")
        ds_cv = ds_ck = ds_aux
        ds_out = S.dsem("ds_out")
        ds_modd = [S.dsem("ds_modd0"), S.dsem("ds_modd1")]
        ds_fo = [S.dsem(f"ds_fo{i}") for i in range(4)]
        ds_xo = [S.dsem(f"ds_xo{t}") for t in range(4)]
        ds_bias = [S.dsem("ds_bias0"), S.dsem("ds_bias1")]
        ds_kvst, ds_gb, ds_h = S.dsem("ds_kvst"), S.dsem("ds_gb"), S.dsem("ds_h")
        ds_est, ds_g2 = S.dsem("ds_est"), S.dsem("ds_g2")
        ds_dbg = S.dsem("ds_dbg")

        wseq = weight_sequence(depth, groups)
        wst = {"load": 0, "use": 0}

        def w_src(desc):
            kind = desc[0]
            if kind == "ada":
                _, l, i, cb = desc
                return w_ada[l, i * 4 + cb, :, :], 8192
            if kind == "in":
                _, l, cbi = desc
                return w_in[l, cbi, :, :], 8192
            if kind == "o":
                _, l, cb = desc
                return w_o[l, cb, :, :], 8192
            if kind == "up":
                _, l, c = desc
                return w_up[l, c, :, :], 8192
            _, l, cb, sub = desc
            k0 = sub * 16
            nk = min(16, NFT - k0)
            return w_down[l, cb, :, k0 * 512:(k0 + nk) * 512], nk * 512

        def w_issue():
            i = wst["load"]
            slot = i % 2
            src, n = w_src(wseq[i])
            dst = wr[slot][:].rearrange("p k c -> p (k c)")
            fns = []
            for c0 in range(0, n, 4096):
                c1 = min(n, c0 + 4096)
                fns.append(lambda e, c0=c0, c1=c1, src=src, dst=dst: e.dma_start(out=dst[:, c0:c1], in_=src[:, c0:c1]))
            S.dma("pool", ds_wr[slot][(i // 2) % 2], fns, writes=[wrb[slot]])
            wst["load"] += 1

        def w_next(desc):
            i = wst["use"]
            assert wseq[i] == desc, (wseq[i], desc)
            while wst["load"] < min(i + 2, len(wseq)):
                w_issue()
            wst["use"] += 1
            return wr[i % 2], wrb[i % 2]

        def check_stop(tag):
            if stop is not None and tuple(stop) == tuple(tag):
                raise StopBuild()

        try:
            S.dma("sp", ds_c, [
                lambda e: e.dma_start(out=idf[:], in_=ident_d[:, :]),
                lambda e: e.dma_start(out=ropec[:], in_=ropec_d[:, :]),
                lambda e: e.dma_start(out=ropes[:], in_=ropes_d[:, :]),
                lambda e: e.dma_start(out=sel[:], in_=sel_d[:, :]),
                lambda e: e.dma_start(out=cT[:], in_=cT_d[:, :]),
            ], writes=[constb, cTb])
            maskb = auxb
            S.dma("pool", ds_aux, [lambda e: e.dma_start(out=masks[:], in_=masks_d[:, :])], writes=[maskb])
            S.op("dve", lambda e: e.tensor_copy(out=idb[:], in_=idf[:]), reads=[constb, maskb], writes=[constb])
            S.op("dve", lambda e: e.memset(ones_bf[:], 1.0), writes=[constb])
            S.op("act", lambda e: e.activation(out=sT[:].rearrange("p k m -> p (k m)"), in_=cT[:], func=AF.Silu),
                 reads=[cTb], writes=[sTb])

            rowt = [XH[0:33, 0:512], XH[0:33, 512:1024]]
            biast = [XH[0:33, 1024:1536], XH[0:33, 1536:2048]]
            rowb = [Buf("rowt0"), Buf("rowt1")]
            biasb = [Buf("biast0"), Buf("biast1")]
            S.op("dve", lambda e: e.memset(XH[0:33, :], 0.0), writes=biasb + rowb)
            ctr["nrow"] = 0

            def ada_block(l, i, cb, bank, bb):
                wt, wb_ = w_next(("ada", l, i, cb))
                r = ctr["nrow"] % 2
                ctr["nrow"] += 1
                c0 = i * D + cb * 512
                S.dma("sp", ds_bias[r], [
                    lambda e: e.dma_start(out=biast[r][0:1, :], in_=b_ada[l:l + 1, c0:c0 + 512]),
                    lambda e: e.dma_start(out=biast[r][32:33, :], in_=b_ada[l:l + 1, c0:c0 + 512]),
                ], writes=[biasb[r]])
                S.group("pe", [
                    (lambda e, k=k: e.matmul(bank[0:33, :], lhsT=sT[:, k, :], rhs=wt[:, k, :], start=(k == 0), stop=(k == 15)))
                    for k in range(16)], reads=[sTb, wb_], writes=[bb])
                S.op("dve", lambda e: e.tensor_tensor(out=rowt[r], in0=bank[0:33, :], in1=biast[r], op=ALU.add),
                     reads=[bb, biasb[r]], writes=[rowb[r]])
                S.dma("sp", ds_modd[r], [
                    lambda e: e.dma_start(out=modd[l, 0:1, c0:c0 + 512], in_=rowt[r][0:1, :]),
                    lambda e: e.dma_start(out=modd[l, 1:2, c0:c0 + 512], in_=rowt[r][32:33, :]),
                ], reads=[rowb[r]], writes=[moddb2[r]])

            for i in range(6):
                for cb in range(4):
                    bank, bb = next_mm()
                    ada_block(0, i, cb, bank, bb)
            S.fence([XHb], rowb + biasb)
            check_stop(("ada",))

            def ln_stats(x_ap, xbuf):
                i = nxt("stat", 4)
                s_t, s_b, m_t, m_b = stt[i], sttb[i], mv[i], mvb[i]
                for c4 in range(4):
                    S.op("dve", lambda e, c4=c4: e.bn_stats(out=s_t[:, c4, :], in_=x_ap[:, c4 * 512:(c4 + 1) * 512]),
                         reads=[xbuf], writes=[s_b])
                S.op("dve", lambda e: e.bn_aggr(out=m_t[:, 0:2], in_=s_t[:].rearrange("p a b -> p (a b)")),
                     reads=[s_b], writes=[m_b])
                S.op("dve", lambda e: e.tensor_scalar(out=m_t[:, 2:3], in0=m_t[:, 1:2], scalar1=EPS, scalar2=None, op0=ALU.add),
                     reads=[m_b], writes=[m_b])
                S.op("act", lambda e: e.activation(out=m_t[:, 3:4], in_=m_t[:, 2:3], func=AF.Sqrt), reads=[m_b], writes=[m_b])
                S.op("dve", lambda e: e.reciprocal(out=m_t[:, 4:5], in_=m_t[:, 3:4]), reads=[m_b], writes=[m_b])
                S.op("dve", lambda e: e.tensor_scalar(out=m_t[:, 5:6], in0=m_t[:, 0:1], scalar1=m_t[:, 4:5], scalar2=-1.0,
                                                      op0=ALU.mult, op1=ALU.mult), reads=[m_b], writes=[m_b])
                return m_t[:, 4:5], m_t[:, 5:6], m_b

            def ln_stats_all():
                i = nxt("statA", 2)
                s_t, s_b, m_t, m_b = sttA[i], sttAb[i], mvA[i], mvAb[i]
                for t in range(4):
                    for c4 in range(4):
                        S.op("dve", lambda e, t=t, c4=c4: e.bn_stats(out=s_t[:, t, c4, :], in_=X[:, t, c4 * 512:(c4 + 1) * 512]),
                             reads=[Xb[t]], writes=[s_b])
                for t in range(4):
                    S.op("dve", lambda e, t=t: e.bn_aggr(out=m_t[:, t, 0:2], in_=s_t[:, t, :, :].rearrange("p a b -> p (a b)")),
                         reads=[s_b], writes=[m_b])
                S.op("dve", lambda e: e.tensor_scalar(out=m_t[:, :, 2], in0=m_t[:, :, 1], scalar1=EPS, scalar2=None, op0=ALU.add),
                     reads=[m_b], writes=[m_b])
                S.op("act", lambda e: e.activation(out=m_t[:, :, 3], in_=m_t[:, :, 2], func=AF.Sqrt), reads=[m_b], writes=[m_b])
                S.op("dve", lambda e: e.reciprocal(out=m_t[:, :, 4], in_=m_t[:, :, 3]), reads=[m_b], writes=[m_b])
                S.op("dve", lambda e: e.scalar_tensor_tensor(out=m_t[:, :, 5], in0=m_t[:, :, 0], scalar=-1.0, in1=m_t[:, :, 4],
                                                             op0=ALU.mult, op1=ALU.mult), reads=[m_b], writes=[m_b])
                return [(m_t[:, t, 4:5], m_t[:, t, 5:6], m_b) for t in range(4)]

            def ln_to_hT_all(sh_c, sc_c):
                sts = ln_stats_all()
                for t in range(4):
                    ln_to_hT(t, sh_c, sc_c, sts[t])

            def ln_to_hT(t, sh_c, sc_c, st_=None):
                x_ap = X[:, t, :]
                rstd, nmr, m_b = st_ if st_ is not None else ln_stats(x_ap, Xb[t])
                S.op("act", lambda e: e.activation(out=XH[:], in_=x_ap, func=AF.Identity, scale=rstd, bias=nmr),
                     reads=[Xb[t], m_b], writes=[XHb])
                for k4 in range(4):
                    bank, bb = next_mm()
                    S.group("pe", [
                        (lambda e, kk=kk, bank=bank, k4=k4: e.transpose(out=bank[:, kk * 128:(kk + 1) * 128],
                                                                       in_=XH[:, (k4 * 4 + kk) * 128:(k4 * 4 + kk + 1) * 128],
                                                                       identity=idf[:]))
                        for kk in range(4)], reads=[XHb, constb], writes=[bb])
                    for kk in range(4):
                        k = k4 * 4 + kk
                        if k4 % 2 == 0:
                            S.op("act", lambda e, kk=kk, k=k, bank=bank: e.activation(
                                out=hT[:, k, t * 128:(t + 1) * 128], in_=bank[:, kk * 128:(kk + 1) * 128], func=AF.Identity,
                                scale=modT[:, sc_c + k:sc_c + k + 1], bias=modT[:, sh_c + k:sh_c + k + 1]),
                                reads=[bb, modTb], writes=[hTb[t]])
                        else:
                            S.op("dve", lambda e, kk=kk, k=k, bank=bank: e.tensor_scalar(
                                out=hT[:, k, t * 128:(t + 1) * 128], in0=bank[:, kk * 128:(kk + 1) * 128],
                                scalar1=modT[:, sc_c + k:sc_c + k + 1], scalar2=modT[:, sh_c + k:sh_c + k + 1],
                                op0=ALU.mult, op1=ALU.add), reads=[bb, modTb], writes=[hTb[t]])

            def transpose_block(tmp_t, tmp_b, nblk, dst_ap, dst_bufs, eng="dve"):
                trt, trb_ = next_tr()
                S.group("pe", [
                    (lambda e, i=i: e.transpose(out=trt[:, i * 128:(i + 1) * 128], in_=tmp_t[:, i * 128:(i + 1) * 128],
                                                identity=idb[:]))
                    for i in range(nblk)], reads=[tmp_b, constb], writes=[trb_])
                src = trt[:, 0:nblk * 128].rearrange("p (h t) -> p h t", h=nblk)
                if eng == "dve":
                    S.op("dve", lambda e: e.tensor_copy(out=dst_ap, in_=src), reads=[trb_], writes=dst_bufs)
                else:
                    S.op("act", lambda e: e.activation(out=dst_ap, in_=src, func=AF.Copy), reads=[trb_], writes=dst_bufs)

            def rms_norm(bank, bb, nh, g_t, g_b, out_ap, out_bufs):
                sq, sqb = next_ft()
                S.op("act", lambda e: e.activation(out=sq[:, 0:nh * 128], in_=bank[:, 0:nh * 128], func=AF.Square),
                     reads=[bb], writes=[sqb])
                S.op("dve", lambda e: e.tensor_reduce(out=ss[:, 0:nh], in_=sq[:, 0:nh * 128].rearrange("p (h d) -> p h d", h=nh),
                                                      axis=AX.X, op=ALU.add), reads=[sqb], writes=[ssb])
                S.op("dve", lambda e: e.tensor_scalar(out=ss[:, 4:4 + nh], in0=ss[:, 0:nh], scalar1=1.0 / HD, scalar2=EPS,
                                                      op0=ALU.mult, op1=ALU.add), reads=[ssb], writes=[ssb])
                S.op("act", lambda e: e.activation(out=ss[:, 8:8 + nh], in_=ss[:, 4:4 + nh], func=AF.Sqrt), reads=[ssb], writes=[ssb])
                S.op("dve", lambda e: e.reciprocal(out=ss[:, 12:12 + nh], in_=ss[:, 8:8 + nh]), reads=[ssb], writes=[ssb])
                for h in range(nh):
                    S.op("dve", lambda e, h=h: e.scalar_tensor_tensor(
                        out=out_ap[:, h * 128:(h + 1) * 128], in0=bank[:, h * 128:(h + 1) * 128], scalar=ss[:, 12 + h:13 + h],
                        in1=g_t[:], op0=ALU.mult, op1=ALU.mult), reads=[bb, ssb, g_b], writes=out_bufs)

            def rope(src_ap, src_bufs, nh, t, dst_ap, dst_bufs):
                xv = src_ap.rearrange("p (h a b f) -> p h a b f", h=nh, a=2, b=2)
                ov = dst_ap.rearrange("p (h a b f) -> p h a b f", h=nh, a=2, b=2)
                x1, x2 = xv[:, :, :, 0, :], xv[:, :, :, 1, :]
                C = ropec[:, t * 64:(t + 1) * 64].rearrange("p (a f) -> p a f", a=2).unsqueeze(1).broadcast_to([128, nh, 2, 32])
                Sn = ropes[:, t * 64:(t + 1) * 64].rearrange("p (a f) -> p a f", a=2).unsqueeze(1).broadcast_to([128, nh, 2, 32])
                fa, fab = next_ft()
                fb_, fbb = next_ft()
                n = nh * 64
                t1 = fa[:, 0:n].rearrange("p (h a f) -> p h a f", h=nh, a=2)
                t2 = fa[:, 256:256 + n].rearrange("p (h a f) -> p h a f", h=nh, a=2)
                t3 = fb_[:, 0:n].rearrange("p (h a f) -> p h a f", h=nh, a=2)
                t4 = fb_[:, 256:256 + n].rearrange("p (h a f) -> p h a f", h=nh, a=2)
                S.op("dve", lambda e: e.tensor_tensor(out=t1, in0=x1, in1=C, op=ALU.mult), reads=src_bufs + [constb], writes=[fab])
                S.op("dve", lambda e: e.tensor_tensor(out=t2, in0=x2, in1=Sn, op=ALU.mult), reads=src_bufs + [constb], writes=[fab])
                S.op("dve", lambda e: e.tensor_tensor(out=t3, in0=x2, in1=C, op=ALU.mult), reads=src_bufs + [constb], writes=[fbb])
                S.op("dve", lambda e: e.tensor_tensor(out=t4, in0=x1, in1=Sn, op=ALU.mult), reads=src_bufs + [constb], writes=[fbb])
                S.op("dve", lambda e: e.tensor_tensor(out=ov[:, :, :, 0, :], in0=t1, in1=t2, op=ALU.subtract), reads=[fab], writes=dst_bufs)
                S.op("dve", lambda e: e.tensor_tensor(out=ov[:, :, :, 1, :], in0=t3, in1=t4, op=ALU.add), reads=[fbb], writes=dst_bufs)

            def load_layer_consts(l, sidx, lat):
                S.dma("sp", ds_mod, [lambda e: e.dma_start(out=mod_in[:], in_=modd[l, sidx, :].rearrange("(i p) -> i p", p=128))],
                      reads=moddb2, writes=[mod_inb])
                bank, bb = next_mm()
                S.group("pe", [lambda e: e.transpose(out=bank[:, 0:96], in_=mod_in[:], identity=idf[0:96, 0:96])],
                        reads=[mod_inb, constb], writes=[bb])
                S.op("dve", lambda e: e.tensor_copy(out=modT[:], in_=bank[:, 0:96]), reads=[bb], writes=[modTb])
                S.op("dve", lambda e: e.tensor_scalar(out=modT[:, 16:32], in0=modT[:, 16:32], scalar1=1.0, scalar2=None, op0=ALU.add),
                     reads=[modTb], writes=[modTb])
                S.op("dve", lambda e: e.tensor_scalar(out=modT[:, 64:80], in0=modT[:, 64:80], scalar1=1.0, scalar2=None, op0=ALU.add),
                     reads=[modTb], writes=[modTb])
                S.dma("sp", ds_cw, [lambda e: e.dma_start(out=cw[:], in_=cw_d[l, :, :])], writes=[cwb])
                S.dma("sp", ds_gq, [lambda e: e.dma_start(out=gq_b[:], in_=qg_d[l, :].partition_broadcast(128))], writes=[gqb])
                S.dma("sp", ds_gk, [lambda e: e.dma_start(out=gk_b[:], in_=kg_d[l, :].partition_broadcast(128))], writes=[gkb])
                S.dma("sp", ds_sink, [lambda e: e.dma_start(out=esink[:], in_=sink_d[l, :].partition_broadcast(128))], writes=[esinkb])
                S.op("act", lambda e: e.activation(out=esink[:], in_=esink[:], func=AF.Exp), reads=[esinkb], writes=[esinkb])
                if lat:
                    for m in range(2):
                        for tt in range(2):
                            S.dma("pool", ds_cv, [lambda e, m=m, tt=tt: e.dma_start(
                                out=cv_sb[:, tt, m, :], in_=cch[l, m, 1, tt * 128:(tt + 1) * 128, :])], writes=[cvb])
                            S.dma("pool", ds_ck, [lambda e, m=m, tt=tt: e.dma_start(
                                out=ck_sb[:, tt, m, :], in_=cch[l, m, 0, tt * 128:(tt + 1) * 128, :])], writes=[ckb])
                    for tt in range(2):
                        transpose_block(ck_sb[:, tt, :, :].rearrange("p m c -> p (m c)"), ckb, 4,
                                        ckT[:, :, tt * 128:(tt + 1) * 128], [ckTb], eng="act")

            def load_bcast(l, sidx, which):
                gi_ = 2 if which == 1 else 5
                g_d, b_d = (ln1g_d, ln1b_d) if which == 1 else (ln2g_d, ln2b_d)
                S.dma("sp", ds_gt, [lambda e: e.dma_start(out=GT[:], in_=modd[l, sidx, gi_ * D:(gi_ + 1) * D].partition_broadcast(128))],
                      reads=moddb2, writes=[GTb])
                S.dma("sp", ds_lg, [lambda e: e.dma_start(out=LG[:], in_=g_d[l, :].partition_broadcast(128))], writes=[LGb])
                S.dma("sp", ds_lb, [lambda e: e.dma_start(out=LB[:], in_=b_d[l, :].partition_broadcast(128))], writes=[LBb])

            def resid_update(bank, bb, t, cb, first=True):
                tmp, tmpb_ = next_ft()
                xs = X[:, t, cb * 512:(cb + 1) * 512]
                S.op("dve", lambda e: e.tensor_tensor(out=tmp[:], in0=bank[:], in1=GT[:, cb * 512:(cb + 1) * 512], op=ALU.mult),
                     reads=[bb, GTb], writes=[tmpb_])
                S.op("dve", lambda e: e.scalar_tensor_tensor(out=xs, in0=xs, scalar=(ALPHA if first else 1.0), in1=tmp[:],
                                                             op0=ALU.mult, op1=ALU.add), reads=[tmpb_, Xb[t]], writes=[Xb[t]])

            def post_norm_all():
                sts = ln_stats_all()
                for t in range(4):
                    rstd, nmr, m_b = sts[t]
                    S.op("act", lambda e, t=t, rstd=rstd, nmr=nmr: e.activation(out=X[:, t, :], in_=X[:, t, :], func=AF.Identity, scale=rstd, bias=nmr),
                         reads=[Xb[t], m_b], writes=[Xb[t]])
                for t in range(4):
                    S.op("dve", lambda e, t=t: e.tensor_tensor(out=X[:, t, :], in0=X[:, t, :], in1=LG[:], op=ALU.mult), reads=[Xb[t], LGb], writes=[Xb[t]])
                    S.op("dve", lambda e, t=t: e.tensor_tensor(out=X[:, t, :], in0=X[:, t, :], in1=LB[:], op=ALU.add), reads=[Xb[t], LBb], writes=[Xb[t]])

            def attn_unit(uidx, q_ap, q_bufs, keys, sink_h0, o_dst, o_bufs):
                ob, obb = (att[0], attb[0]) if uidx % 2 == 0 else (mm[2], mmb[2])
                db, dbb = (att[1], attb[1]) if uidx % 2 == 0 else (mm[3], mmb[3])
                n = len(keys)
                pts = {}

                def issue_S(i):
                    si = nxt("mm", 2)
                    sbk, sbb = mm[si], mmb[si]
                    ky = keys[i]
                    S.group("pe", [lambda e: e.matmul(sbk[:], lhsT=ky["kT"], rhs=q_ap, start=True, stop=True)],
                            reads=q_bufs + ky["bufs"], writes=[sbb])
                    pt, ptb = next_pT()
                    S.op("act", lambda e: e.activation(out=pt[:], in_=sbk[:], func=AF.Exp, scale=SCALE), reads=[sbb], writes=[ptb])
                    if ky.get("mask") is not None:
                        mk = ky["mask"].unsqueeze(1).broadcast_to([128, 4, 128])
                        pv = pt[:].rearrange("p (h q) -> p h q", h=4)
                        S.op("dve", lambda e: e.tensor_tensor(out=pv, in0=pv, in1=mk, op=ALU.mult), reads=[ptb, constb], writes=[ptb])
                    pts[i] = (pt, ptb)

                def issue_PV(i):
                    pt, ptb = pts.pop(i)
                    ky = keys[i]
                    S.group("pe", [
                        lambda e: e.matmul(ob[:], lhsT=ky["v"], rhs=pt[:], start=(i == 0), stop=(i == n - 1)),
                        lambda e: e.matmul(db[:], lhsT=ones_bf[:], rhs=pt[:], start=(i == 0), stop=(i == n - 1)),
                    ], reads=[ptb, constb] + ky["bufs"], writes=[obb, dbb])

                issue_S(0)
                if n > 1:
                    issue_S(1)
                for i in range(n):
                    issue_PV(i)
                    if i + 2 < n:
                        issue_S(i + 2)
                rec, recb = next_ft()
                if sink_h0 is not None:
                    for h in range(4):
                        S.op("dve", lambda e, h=h: e.tensor_scalar(out=rec[:, h * 128:(h + 1) * 128], in0=db[:, h * 128:(h + 1) * 128],
                                                                   scalar1=esink[:, sink_h0 + h:sink_h0 + h + 1], scalar2=None, op0=ALU.add),
                             reads=[dbb, esinkb], writes=[recb])
                    S.op("dve", lambda e: e.reciprocal(out=rec[:], in_=rec[:]), reads=[recb], writes=[recb])
                else:
                    S.op("dve", lambda e: e.reciprocal(out=rec[:], in_=db[:]), reads=[dbb], writes=[recb])
                S.op("dve", lambda e: e.tensor_tensor(out=o_dst, in0=ob[:].rearrange("p (h q) -> p h q", h=4),
                                                      in1=rec[:].rearrange("p (h q) -> p h q", h=4), op=ALU.mult),
                     reads=[obb, recb], writes=o_bufs)

            def group_layer(gi, l, lat):
                sidx = 1 if lat else 0
                last = (l == depth - 1)
                xsrc = xl if lat else xc
                ydst = yl if lat else yc
                row0 = 0 if lat else gi * 512
                tag = lambda p: ("gl", gi, l, p)

                load_layer_consts(l, sidx, lat)
                for t in range(4):
                    if l == 0:
                        S.dma("sp", ds_x[t], [lambda e, t=t: e.dma_start(out=X[:, t, :], in_=xsrc[row0 + t * 128:row0 + (t + 1) * 128, :])],
                              writes=[Xb[t]])
                ln_to_hT_all(0, 16)
                check_stop(tag("P1"))

                S.fence(qTb + [GBb, Hb], aTb)
                pending = []

                def flush_pending():
                    while pending:
                        pending.pop(0)()

                for cbi in QKV_ORDER:
                    wt, wb_ = w_next(("in", l, cbi))
                    for t in range(4):
                        bank, bb = next_mm()
                        S.group("pe", [
                            (lambda e, k=k, bank=bank, wt=wt, t=t: e.matmul(bank[:], lhsT=hT[:, k, t * 128:(t + 1) * 128],
                                                                           rhs=wt[:, k, :], start=(k == 0), stop=(k == 15)))
                            for k in range(16)], reads=[hTb[t], wb_], writes=[bb])
                        flush_pending()
                        tsl = slice(t * 128, (t + 1) * 128)
                        is_q = cbi in (0, 1, 3, 4)
                        is_b = cbi >= 3
                        if is_q:
                            h0 = {0: 0, 1: 4, 3: 8, 4: 12}[cbi]
                            tb_, tbb = next_tmpb()
                            if not is_b:
                                if lat:
                                    rope(bank[:], [bb], 4, t, tb_[:], [tbb])
                                else:
                                    S.op("act", lambda e, tb_=tb_, bank=bank: e.activation(out=tb_[:], in_=bank[:], func=AF.Copy),
                                         reads=[bb], writes=[tbb])
                            else:
                                if lat:
                                    nr, nrb = next_ft()
                                    rms_norm(bank, bb, 4, gq_b, gqb, nr[:], [nrb])
                                    rope(nr[:], [nrb], 4, t, tb_[:], [tbb])
                                else:
                                    rms_norm(bank, bb, 4, gq_b, gqb, tb_[:], [tbb])
                            pending.append(lambda tb_=tb_, tbb=tbb, h0=h0, tsl=tsl, t=t: transpose_block(
                                tb_, tbb, 4, qT_v[:, h0:h0 + 4, tsl], [qTb[t]], eng="act"))
                        else:
                            kvh0 = 2 if is_b else 0
                            ncx = ncb if is_b else nca
                            tb_, tbb = next_tmpb()
                            S.op("act", lambda e, bank=bank, kvh0=kvh0, t=t: e.activation(
                                out=vv[:, kvh0:kvh0 + 2, t, :], in_=bank[:, 256:512].rearrange("p (h d) -> p h d", h=2), func=AF.Copy),
                                reads=[bb], writes=[vb[t]])
                            if lat:
                                if is_b:
                                    nr, nrb = next_ft()
                                    rms_norm(bank, bb, 2, gk_b, gkb, nr[:, 0:256], [nrb])
                                    rope(nr[:, 0:256], [nrb], 2, t, tb_[:, 0:256], [tbb])
                                else:
                                    rope(bank[:, 0:256], [bb], 2, t, tb_[:, 0:256], [tbb])
                            else:
                                kvo, kvob = next_ft()
                                if is_b:
                                    rms_norm(bank, bb, 2, gk_b, gkb, kvo[:, 0:256], [kvob])
                                    S.op("act", lambda e, kvo=kvo, bank=bank: e.activation(out=kvo[:, 256:512], in_=bank[:, 256:512], func=AF.Copy),
                                         reads=[bb], writes=[kvob])
                                else:
                                    S.op("act", lambda e, kvo=kvo, bank=bank: e.activation(out=kvo[:], in_=bank[:], func=AF.Copy),
                                         reads=[bb], writes=[kvob])
                                sq_ = gi * 2 + t // 2
                                r0 = (t % 2) * 128
                                out_toks.append(S.dma("sp", ds_fo[ft_idx[id(kvo)]], [
                                    lambda e, kvo=kvo, ncx=ncx, sq_=sq_, r0=r0: e.dma_start(out=ncx[sq_, l, 0, r0:r0 + 128, :], in_=kvo[:, 0:256]),
                                    lambda e, kvo=kvo, ncx=ncx, sq_=sq_, r0=r0: e.dma_start(out=ncx[sq_, l, 1, r0:r0 + 128, :], in_=kvo[:, 256:512]),
                                ], reads=[kvob], writes=[outb]))
                                S.op("dve", lambda e, kvo=kvo, tb_=tb_: e.tensor_copy(out=tb_[:, 0:256], in_=kvo[:, 0:256]),
                                     reads=[kvob], writes=[tbb])
                            pending.append(lambda tb_=tb_, tbb=tbb, kvh0=kvh0, tsl=tsl, t=t: transpose_block(
                                tb_, tbb, 2, kT[:, kvh0:kvh0 + 2, tsl], [kTb[t]], eng="act"))
                    if lat and cbi == 5:
                        flush_pending()
                    if lat and cbi == 5:
                        S.dma("sp", ds_kvst, [
                            lambda e: e.dma_start(out=kvst[:, 0:2048], in_=kT[:].rearrange("p h t -> p (h t)")),
                            lambda e: e.dma_start(out=kvst[:, 2048:4096], in_=vv[:].rearrange("p h t d -> p (h t d)")),
                        ], reads=kTb + vb, writes=[kvstb])
                        ctr["cc"] += 1
                        S.custom("pool", lambda e, v=ctr["cc"]: e.collective_compute(
                            "AllGather", ALU.bypass, replica_groups=RG, ins=[kvst_t.ap().opt()], outs=[kvg_t.ap().opt()]).then_inc(ccsem, 1),
                            ccsem, ctr["cc"], reads=[kvstb], writes=[kvgb])
                        kvg_v = kvg.rearrange("(r p) x -> p r x", p=128)
                        S.dma("sp", ds_gb, [
                            lambda e: e.dma_start(out=GB_v[:, :, 0:1024], in_=kvg_v[:, :, 1024:2048]),
                            lambda e: e.dma_start(out=GB_v[:, :, 1024:2048], in_=kvg_v[:, :, 3072:4096]),
                        ], reads=[kvgb], writes=[GBb])
                        kA = kvg_v[:, :, 0:1024].rearrange("p r (h t) -> p r h t", h=2)
                        vA = kvg_v[:, :, 2048:3072].rearrange("p r (h tt d) -> p r h tt d", h=2, tt=4)
                        hfns = []
                        for h_ in range(2):
                            hfns += [
                                lambda e, h_=h_: e.dma_start(out=H_v[:, :, 0 + h_ * 128:0 + (h_ + 1) * 128], in_=kA[:, :, h_, 384:512]),
                                lambda e, h_=h_: e.dma_start(out=H_v[:, :, 256 + h_ * 128:256 + (h_ + 1) * 128], in_=vA[:, :, h_, 3, :]),
                                lambda e, h_=h_: e.dma_start(out=H_v[:, :, 512 + h_ * 128:512 + (h_ + 1) * 128], in_=kA[:, :, h_, 0:128]),
                                lambda e, h_=h_: e.dma_start(out=H_v[:, :, 768 + h_ * 128:768 + (h_ + 1) * 128], in_=vA[:, :, h_, 0, :]),
                            ]
                        S.dma("sp", ds_h, hfns, reads=[kvgb], writes=[Hb])
                        for side in range(2):
                            hs = HS[:, side * 512:(side + 1) * 512]
                            for r in range(4):
                                src = H_v[:, r, side * 512:(side + 1) * 512]
                                sc_ = sel[:, side * 4 + r:side * 4 + r + 1]
                                if r == 0:
                                    S.op("dve", lambda e, hs=hs, src=src, sc_=sc_: e.tensor_scalar(out=hs, in0=src, scalar1=sc_, scalar2=None, op0=ALU.mult),
                                         reads=[Hb, constb], writes=[HSb])
                                else:
                                    S.op("dve", lambda e, hs=hs, src=src, sc_=sc_: e.scalar_tensor_tensor(out=hs, in0=src, scalar=sc_, in1=hs, op0=ALU.mult, op1=ALU.add),
                                         reads=[Hb, constb, HSb], writes=[HSb])
                flush_pending()
                check_stop(tag("P2"))

                load_bcast(l, sidx, 1)
                u = 0
                if not lat:
                    for s_ in range(2):
                        for g in range(4):
                            for qt in range(2):
                                t = s_ * 2 + qt
                                keys = [dict(kT=kT[:, g, (s_ * 2 + kt) * 128:(s_ * 2 + kt + 1) * 128], v=vv[:, g, s_ * 2 + kt, :],
                                             bufs=[kTb[s_ * 2 + kt], vb[s_ * 2 + kt]]) for kt in range(2)]
                                attn_unit(u, qT_v[:, 4 * g:4 * g + 4, t * 128:(t + 1) * 128], [qTb[t]], keys,
                                          (4 * g if g < 2 else None), hT[:, 4 * g:4 * g + 4, t * 128:(t + 1) * 128], [hTb[t]])
                                u += 1
                else:
                    triP, triN, mL, mR = (masks[:, i * 128:(i + 1) * 128] for i in range(4))
                    unit_order = ([(g, qt) for g in (0, 1) for qt in (1, 2)] + [(g, qt) for g in (0, 1) for qt in (0, 3)]
                                  + [(g, qt) for g in (2, 3) for qt in range(4)])
                    for (g, qt) in unit_order:
                        if True:
                            keys = []
                            if g < 2:
                                if qt == 0:
                                    keys.append(dict(kT=HS[:, g * 128:(g + 1) * 128], v=HS[:, 256 + g * 128:256 + (g + 1) * 128], mask=mL, bufs=[HSb]))
                                else:
                                    keys.append(dict(kT=kT[:, g, (qt - 1) * 128:qt * 128], v=vv[:, g, qt - 1, :], mask=triP, bufs=[kTb[qt - 1], vb[qt - 1]]))
                                keys.append(dict(kT=kT[:, g, qt * 128:(qt + 1) * 128], v=vv[:, g, qt, :], bufs=[kTb[qt], vb[qt]]))
                                if qt == 3:
                                    keys.append(dict(kT=HS[:, 512 + g * 128:512 + (g + 1) * 128], v=HS[:, 768 + g * 128:768 + (g + 1) * 128], mask=mR, bufs=[HSb]))
                                else:
                                    keys.append(dict(kT=kT[:, g, (qt + 1) * 128:(qt + 2) * 128], v=vv[:, g, qt + 1, :], mask=triN, bufs=[kTb[qt + 1], vb[qt + 1]]))
                            else:
                                hB = g - 2
                                for r in range(4):
                                    for tt in range(4):
                                        keys.append(dict(kT=GB_v[:, r, hB * 512 + tt * 128:hB * 512 + (tt + 1) * 128],
                                                         v=GB_v[:, r, 1024 + hB * 512 + tt * 128:1024 + hB * 512 + (tt + 1) * 128], bufs=[GBb]))
                            for tt in range(2):
                                keys.append(dict(kT=ckT[:, g, tt * 128:(tt + 1) * 128], v=cv_sb[:, tt, g // 2, (g % 2) * 128:(g % 2 + 1) * 128],
                                                 bufs=[ckTb, cvb]))
                            attn_unit(u, qT_v[:, 4 * g:4 * g + 4, qt * 128:(qt + 1) * 128], [qTb[qt]], keys,
                                      (4 * g if g < 2 else None), hT[:, 4 * g:4 * g + 4, qt * 128:(qt + 1) * 128], [hTb[qt]])
                            u += 1
                check_stop(tag("P3"))

                for cb in range(4):
                    wt, wb_ = w_next(("o", l, cb))
                    for t in range(4):
                        bank, bb = next_mm()
                        S.group("pe", [
                            (lambda e, k=k, bank=bank, wt=wt, t=t: e.matmul(bank[:], lhsT=hT[:, k, t * 128:(t + 1) * 128],
                                                                           rhs=wt[:, k, :], start=(k == 0), stop=(k == 15)))
                            for k in range(16)], reads=[hTb[t], wb_], writes=[bb])
                        resid_update(bank, bb, t, cb)
                post_norm_all()
                ln_to_hT_all(48, 64)
                check_stop(tag("P4"))

                load_bcast(l, sidx, 2)
                S.fence(aTb, qTb + [GBb, Hb])
                valc = [XH[:, i * 512:(i + 1) * 512] for i in range(4)]
                valb = [Buf(f"valc{i}") for i in range(4)]
                S.fence(valb, [XHb])
                nseg = 1 if lat else 2
                seglen = 512 // nseg

                def conv(bank, bb, jj, dst, dstb):
                    w0 = cw[:, jj * 4 + 0:jj * 4 + 1]
                    w1 = cw[:, jj * 4 + 1:jj * 4 + 2]
                    w2 = cw[:, jj * 4 + 2:jj * 4 + 3]
                    b_ = cw[:, jj * 4 + 3:jj * 4 + 4]
                    S.op("act", lambda e: e.activation(out=dst, in_=bank[:], func=AF.Identity, scale=w1, bias=b_),
                         reads=[bb, cwb], writes=[dstb])
                    bv = bank[:].rearrange("p (s t) -> p s t", s=nseg)
                    dv = dst.rearrange("p (s t) -> p s t", s=nseg)
                    S.op("dve", lambda e: e.scalar_tensor_tensor(out=dv[:, :, 1:seglen], in0=bv[:, :, 0:seglen - 1], scalar=w0,
                                                                 in1=dv[:, :, 1:seglen], op0=ALU.mult, op1=ALU.add),
                         reads=[bb, cwb, dstb], writes=[dstb])
                    S.op("dve", lambda e: e.scalar_tensor_tensor(out=dv[:, :, 0:seglen - 1], in0=bv[:, :, 1:seglen], scalar=w2,
                                                                 in1=dv[:, :, 0:seglen - 1], op0=ALU.mult, op1=ALU.add),
                         reads=[bb, cwb, dstb], writes=[dstb])
                    if lat:
                        S.op("act", lambda e: e.activation(out=E_own[:, jj, :], in_=bank[:, 0:512:511], func=AF.Copy), reads=[bb], writes=[Eb])
                        S.op("act", lambda e: e.activation(out=P_edge[:, jj, :], in_=dst[:, 0:512:511], func=AF.Copy), reads=[dstb], writes=[Pb])

                for c in range(11):
                    wt, wb_ = w_next(("up", l, c))
                    for f4 in range(4):
                        bank, bb = next_mm()
                        S.group("pe", [
                            (lambda e, k=k, bank=bank, wt=wt, f4=f4: e.matmul(bank[:], lhsT=wt[:, k, f4 * 128:(f4 + 1) * 128],
                                                                             rhs=hT[:, k, :], start=(k == 0), stop=(k == 15)))
                            for k in range(16)], reads=hTb + [wb_], writes=[bb])
                        conv(bank, bb, c * 4 + f4, valc[f4], valb[f4])
                    wt, wb_ = w_next(("up", l, 11 + c))
                    for f4 in range(4):
                        j = c * 4 + f4
                        bank, bb = next_mm()
                        S.group("pe", [
                            (lambda e, k=k, bank=bank, wt=wt, f4=f4: e.matmul(bank[:], lhsT=wt[:, k, f4 * 128:(f4 + 1) * 128],
                                                                             rhs=hT[:, k, :], start=(k == 0), stop=(k == 15)))
                            for k in range(16)], reads=hTb + [wb_], writes=[bb])
                        gc, gcb = next_ft()
                        conv(bank, bb, NFT + j, gc[:], gcb)
                        S.op("act", lambda e, gc=gc: e.activation(out=gc[:], in_=gc[:], func=AF.Silu), reads=[gcb], writes=[gcb])
                        S.op("dve", lambda e, gc=gc, f4=f4, j=j: e.tensor_tensor(out=aT_v[:, j, :], in0=gc[:], in1=valc[f4], op=ALU.mult),
                             reads=[gcb, valb[f4]], writes=[aTb[j]])
                S.fence([XHb], valb)
                if lat:
                    S.dma("sp", ds_est, [lambda e: e.dma_start(out=est[:, :], in_=E_own[:].rearrange("p j e -> p (j e)"))],
                          reads=[Eb], writes=[estb])
                    ctr["cc"] += 1
                    S.custom("pool", lambda e: e.collective_compute(
                        "AllGather", ALU.bypass, replica_groups=RG, ins=[est_t.ap().opt()], outs=[eg_t.ap().opt()]).then_inc(ccsem, 1),
                        ccsem, ctr["cc"], reads=[estb], writes=[egb])
                    S.dma("sp", ds_g2, [lambda e: e.dma_start(out=G2[:], in_=eg.rearrange("(r p) x -> p r x", p=128))],
                          reads=[egb], writes=[G2b])
                    g2v = G2[:].rearrange("p r (j e) -> p r j e", e=2)
                    for side in range(2):
                        edge = 1 - side
                        dst = hlr[:, side, :]
                        for r in range(4):
                            src = g2v[:, r, :, edge]
                            sc_ = sel[:, side * 4 + r:side * 4 + r + 1]
                            if r == 0:
                                S.op("dve", lambda e, dst=dst, src=src, sc_=sc_: e.tensor_scalar(out=dst, in0=src, scalar1=sc_, scalar2=None, op0=ALU.mult),
                                     reads=[G2b, constb], writes=[hlrb])
                            else:
                                S.op("dve", lambda e, dst=dst, src=src, sc_=sc_: e.scalar_tensor_tensor(out=dst, in0=src, scalar=sc_, in1=dst, op0=ALU.mult, op1=ALU.add),
                                     reads=[G2b, constb, hlrb], writes=[hlrb])
                    cwv = cw[:].rearrange("p (j q) -> p j q", q=4)
                    for side in range(2):
                        wsel = cwv[:, :, 0] if side == 0 else cwv[:, :, 2]
                        S.op("dve", lambda e, side=side, wsel=wsel: e.tensor_tensor(out=etmp[:], in0=hlr[:, side, :], in1=wsel, op=ALU.mult),
                             reads=[hlrb, cwb], writes=[etmpb])
                        S.op("dve", lambda e, side=side: e.tensor_tensor(out=P_edge[:, :, side], in0=P_edge[:, :, side], in1=etmp[:], op=ALU.add),
                             reads=[etmpb, Pb], writes=[Pb])
                    pg = P_edge[:, NFT:2 * NFT, :]
                    S.op("act", lambda e: e.activation(out=pg, in_=pg, func=AF.Silu), reads=[Pb], writes=[Pb])
                    S.op("dve", lambda e: e.tensor_tensor(out=aT_v[:, :, 0:512:511], in0=pg, in1=P_edge[:, 0:NFT, :], op=ALU.mult),
                         reads=[Pb], writes=aTb)
                check_stop(tag("P5"))

                ada_n = [0]
                for cb in range(4):
                    for sub in range(3):
                        wt, wb_ = w_next(("down", l, cb, sub))
                        k0 = sub * 16
                        nk = min(16, NFT - k0)
                        for t in range(4):
                            S.group("pe", [
                                (lambda e, kk=kk, t=t, wt=wt, k0=k0, nk=nk, sub=sub: e.matmul(
                                    mm[t][:], lhsT=aT_v[:, k0 + kk, t * 128:(t + 1) * 128], rhs=wt[:, kk, :],
                                    start=(sub == 0 and kk == 0), stop=(sub == 2 and kk == nk - 1)))
                                for kk in range(nk)], reads=aTb[k0:k0 + nk] + [wb_], writes=[mmb[t]])
                            if sub == 2:
                                resid_update(mm[t], mmb[t], t, cb)
                        if gi == groups[0] and l == 0 and depth > 1:
                            if ada_n[0] == 0:
                                S.fence(rowb + biasb, [XHb])
                            for (i2, cb2) in ada_l1_order()[ada_n[0]:ada_n[0] + 2]:
                                ai = ada_n[0] % 2
                                ada_block(1, i2, cb2, att[ai], attb[ai])
                                ada_n[0] += 1
                if gi == groups[0] and l == 0 and depth > 1:
                    S.fence([XHb], rowb + biasb)
                post_norm_all()
                for t in range(4):
                    if last:
                        out_toks.append(S.dma("sp", ds_xo[t], [lambda e, t=t: e.dma_start(
                            out=ydst[row0 + t * 128:row0 + (t + 1) * 128, :], in_=X[:, t, :])], reads=[Xb[t]], writes=[outb]))
                check_stop(tag("P6"))

            for gi in groups:
                for l in range(depth):
                    group_layer(gi, l, gi == 2)
        except StopBuild:
            pass

        dbg_map = {
            "X": (X, [128, 4 * D], F32, "p t d -> p (t d)", Xb),
            "hT": (hT, [128, 16 * 512], BF16, "p k t -> p (k t)", hTb),
            "R": (R, [128, 24576], BF16, None, qTb + [GBb, Hb] + aTb),
            "kT": (kT, [128, 2048], BF16, "p h t -> p (h t)", kTb),
            "vv": (vv, [128, 2048], BF16, "p h t d -> p (h t d)", vb),
            "modT": (modT, [128, 96], F32, None, [modTb]),
            "HS": (HS, [128, 1024], BF16, None, [HSb]),
        }
        for name in dbg:
            ten, shp, dt, rr, bufs = dbg_map[name]
            d_ap = dout("dbg_" + name, shp, dt)
            src = ten[:] if rr is None else ten[:].rearrange(rr)
            out_toks.append(S.dma("sp", ds_dbg, [lambda e, d_ap=d_ap, src=src: e.dma_start(out=d_ap[:, :], in_=src)], reads=bufs))

        for d in S.all_dsems:
            if d.count > 0:
                S.final_wait("sp", [(d.sem, d.count, "dma")])
        S.final_wait("sp", out_toks)
        with nc.Block() as block:
            S.emit(block)
    return nc


def _rope_tables(j):
    n = np.arange(512 * j, 512 * (j + 1))
    row = (n // GRID_W).astype(np.float32)
    col = (n % GRID_W).astype(np.float32)
    q4 = HD // 4
    freq = (ROPE_THETA ** (-np.arange(q4, dtype=np.float32) / q4)).astype(np.float32)
    ang = np.stack([row[:, None] * freq, col[:, None] * freq], axis=1).astype(np.float32)
    c = np.cos(ang).astype(np.float32).reshape(4, 128, 64).transpose(1, 0, 2).reshape(128, 256)
    s = np.sin(ang).astype(np.float32).reshape(4, 128, 64).transpose(1, 0, 2).reshape(128, 256)
    return np.ascontiguousarray(c), np.ascontiguousarray(s)


def _masks(j):
    s = np.arange(128)[:, None]
    q = np.arange(128)[None, :]
    triP = (q <= s).astype(np.float32)
    triN = (s <= q).astype(np.float32)
    mL = triP if j > 0 else np.zeros_like(triP)
    mR = triN if j < 3 else np.zeros_like(triN)
    return np.ascontiguousarray(np.concatenate([triP, triN, mL, mR], axis=1))


def make_in_maps(inp):
    f = lambda a: np.ascontiguousarray(np.asarray(a, dtype=np.float32))
    x_prompt, x_sample = f(inp["x_prompt"]), f(inp["x_sample"])
    ca, cb_ = f(inp["cache_attn_a"]), f(inp["cache_attn_b"])
    c, c_ctx = f(inp["c"]), f(inp["c_ctx"])
    def blk(w):
        l_, k_, n_ = w.shape
        return np.ascontiguousarray(w.reshape(l_, k_ // 128, 128, n_ // 512, 512).transpose(0, 3, 2, 1, 4).reshape(l_, n_ // 512, 128, (k_ // 128) * 512))

    shared = dict(
        w_ada=blk(f(inp["w_ada"])), b_ada=f(inp["b_ada"]), w_in=blk(f(inp["w_in"])), w_o=blk(f(inp["w_o"])),
        w_up=blk(f(inp["w_up"])), w_down=blk(f(inp["w_down"])), qg=f(inp["q_norm_g"]), kg=f(inp["k_norm_g"]),
        sink=f(inp["sink_a"]), ln1g=f(inp["ln1_g"]), ln1b=f(inp["ln1_b"]), ln2g=f(inp["ln2_g"]), ln2b=f(inp["ln2_b"]),
        ident=np.eye(128, dtype=np.float32),
    )
    conv_w, conv_b = f(inp["conv_w"]), f(inp["conv_b"])
    cwp = np.concatenate([conv_w, conv_b[:, None, :]], axis=1)
    cwp = cwp.reshape(L, 4, 88, 128).transpose(0, 3, 2, 1).reshape(L, 128, 352)
    shared["cw"] = np.ascontiguousarray(cwp)
    maps = []
    for i in range(NCORES):
        b, j = i // 4, i % 4
        m = dict(shared)
        m["xc"] = np.ascontiguousarray(x_prompt[4 * i:4 * i + 4].reshape(1024, D))
        m["xl"] = np.ascontiguousarray(x_sample[b, 512 * j:512 * (j + 1)])
        cc = np.stack([ca[b].reshape(L, 2, 256, 256), cb_[b].reshape(L, 2, 256, 256)], axis=1)
        m["cch"] = np.ascontiguousarray(cc)
        ct = np.zeros((128, 16, 33), np.float32)
        ct[:, :, 0] = c_ctx.reshape(16, 128).T
        ct[:, :, 32] = c[b].reshape(16, 128).T
        m["cT"] = np.ascontiguousarray(ct.reshape(128, 16 * 33))
        rc, rs = _rope_tables(j)
        m["ropec"], m["ropes"] = rc, rs
        m["masks"] = _masks(j)
        sl = np.zeros((128, 8), np.float32)
        if j > 0:
            sl[:, j - 1] = 1.0
        if j < 3:
            sl[:, 4 + j + 1] = 1.0
        m["sel"] = sl
        maps.append(m)
    return maps


_NC_CACHE = {}


def kernel(**inputs):
    maps = make_in_maps(inputs)
    if "nc" not in _NC_CACHE:
        _NC_CACHE["nc"] = build_program()
    nc = _NC_CACHE["nc"]
    res = run_bass_kernel_spmd(nc, maps, core_ids=list(range(NCORES)))
    rs = res.results
    y_prompt = np.concatenate([np.asarray(r["yc"]).reshape(4, SEQ, D) for r in rs], axis=0).astype(np.float32)
    y_sample = np.stack([np.concatenate([np.asarray(rs[b * 4 + j]["yl"]) for j in range(4)], axis=0) for b in range(2)], axis=0).astype(np.float32)
    nca = np.concatenate([np.asarray(r["nca"]) for r in rs], axis=0).reshape(32, L, 2, SEQ, 2, HD).astype(np.float32)
    ncb = np.concatenate([np.asarray(r["ncb"]) for r in rs], axis=0).reshape(32, L, 2, SEQ, 2, HD).astype(np.float32)
    return (y_prompt, y_sample, nca, ncb)
```
